# Optimizing a Trainium2 kernel written in Bass

```python
import jax, jax.numpy as jnp
from jax import lax
import numpy as np

D_MODEL = 1024
BATCH = 4
SEQ = 8192
DEPTH = 4

CTX_LEN = 256
GRID_W = 64
NORM_EPS = 1e-6
LRU_WIDTH = D_MODEL
LRU_BLOCKS = 16
LRU_BLOCK_W = LRU_WIDTH // LRU_BLOCKS
CONV_WIDTH = 4
RGLRU_C = 8.0
GLA_HEADS = 4
GLA_DK = D_MODEL // 2 // GLA_HEADS
GLA_DV = D_MODEL // GLA_HEADS
GLA_GATE_RANK = 16
GLA_TAU = 16.0
GLA_CHUNK = 64
ATT_HEADS = 8
ATT_KV_HEADS = 2
ATT_GROUP = ATT_HEADS // ATT_KV_HEADS
HEAD_DIM = 128
Q_BLOCK = 128
ROPE_THETA = 10000.0
ROPE_AXIS_DIM = HEAD_DIM // 2
D_FF = 4 * D_MODEL
N_BRANCHES = 3
BRANCH_WIDTH = D_MODEL
PROJ_WIDTHS = (LRU_WIDTH, LRU_WIDTH,
               GLA_HEADS * GLA_DK, GLA_HEADS * GLA_DK, GLA_HEADS * GLA_DV, GLA_HEADS * GLA_DV, 2 * GLA_GATE_RANK,
               ATT_HEADS * HEAD_DIM, ATT_KV_HEADS * HEAD_DIM, ATT_KV_HEADS * HEAD_DIM,
               N_BRANCHES * D_MODEL)
PROJ_SPLITS = tuple(sum(PROJ_WIDTHS[:i + 1]) for i in range(len(PROJ_WIDTHS) - 1))
D_PROJ = sum(PROJ_WIDTHS)

kernel_name = 'hybrid_rglru_gla_gqa_diffusion_trunk'


def rms_norm(x, g):
    xf = x.astype(jnp.float32)
    y = xf * lax.rsqrt(jnp.mean(xf * xf, axis=-1, keepdims=True) + NORM_EPS)
    return (y * g.astype(jnp.float32)).astype(x.dtype)


def modulate(x, g, shift, scale):
    return rms_norm(x, g) * (1 + scale) + shift


def centred_dwconv(x, w, b):
    y = lax.conv_general_dilated(x, w[:, None, :].astype(x.dtype), window_strides=(1,),
                                 padding=[((CONV_WIDTH - 1) // 2, CONV_WIDTH // 2)],
                                 dimension_numbers=('NWC', 'WIO', 'NWC'),
                                 feature_group_count=x.shape[-1])
    return y + b


def linear_scan(a, u, h0, reverse):
    def combine(e1, e2):
        a1, b1 = e1
        a2, b2 = e2
        return a1 * a2, a2 * b1 + b2
    a_cum, b_cum = lax.associative_scan(combine, (a, u), reverse=reverse, axis=1)
    return a_cum * h0[:, None, :] + b_cum


def rglru_coeffs(xc, a_w, a_b, x_w, x_b, lam):
    bn, n, w = xc.shape
    xh = xc.reshape(bn, n, LRU_BLOCKS, LRU_BLOCK_W)
    r = jax.nn.sigmoid(jnp.einsum('bnhi,hij->bnhj', xh, a_w).reshape(bn, n, w) + a_b).astype(jnp.float32)
    i = jax.nn.sigmoid(jnp.einsum('bnhi,hij->bnhj', xh, x_w).reshape(bn, n, w) + x_b)
    log_a = -RGLRU_C * r * jax.nn.softplus(-lam.astype(jnp.float32))
    a = jnp.exp(log_a)
    u = jnp.sqrt(-jnp.expm1(2.0 * log_a)) * (i * xc).astype(jnp.float32)
    return a, u


def rglru_branch(x_lat, y_lat, x_ctx, y_ctx, conv_w, conv_b, a_w, a_b, x_w, x_b, lam, ctx_out):
    xc_lat = centred_dwconv(x_lat, conv_w, conv_b)
    xc_ctx = centred_dwconv(x_ctx, conv_w, conv_b)
    bn, _, w = xc_ctx.shape
    h_lat = []
    h_ctx = []
    for d in range(2):
        rev = d == 1
        a_c, u_c = rglru_coeffs(xc_ctx, a_w[d], a_b[d], x_w[d], x_b[d], lam[d])
        hc = linear_scan(a_c, u_c, jnp.zeros((bn, w), jnp.float32), rev)
        h_end = hc[:, 0] if rev else hc[:, -1]
        a_l, u_l = rglru_coeffs(xc_lat, a_w[d], a_b[d], x_w[d], x_b[d], lam[d])
        h_lat.append(linear_scan(a_l, u_l, h_end, rev))
        h_ctx.append(hc)
    out_lat = ((h_lat[0] + h_lat[1]) * jax.nn.gelu(y_lat.astype(jnp.float32))).astype(x_lat.dtype)
    out_ctx = None
    if ctx_out:
        out_ctx = ((h_ctx[0] + h_ctx[1]) * jax.nn.gelu(y_ctx.astype(jnp.float32))).astype(x_ctx.dtype)
    return out_lat, out_ctx


def gla_chunked(q, k, v, g, s0, with_output):
    bn, n, h, dk = q.shape
    dv = v.shape[-1]
    nc = n // GLA_CHUNK

    def chunks(t):
        return t.reshape(bn, nc, GLA_CHUNK, h, t.shape[-1]).transpose(1, 0, 3, 2, 4)

    qc, kc, vc, gc = chunks(q), chunks(k), chunks(v), chunks(g)
    b = jnp.cumsum(gc, axis=3)
    b_last = b[:, :, :, -1:, :]
    u = jnp.einsum('nbhcd,nbhce->nbhde', kc * jnp.exp(b_last - b), vc)
    decay = jnp.exp(b_last[:, :, :, 0, :])

    def step(s, inp):
        dec, uu = inp
        return dec[..., None] * s + uu, s

    s_final, s_prev = lax.scan(step, s0, (decay, u))
    if not with_output:
        return None, s_final
    b_mid = b[:, :, :, GLA_CHUNK // 2 - 1:GLA_CHUNK // 2, :]
    scores = jnp.einsum('nbhid,nbhjd->nbhij', qc * jnp.exp(b - b_mid), kc * jnp.exp(b_mid - b))
    mask = jnp.tril(jnp.ones((GLA_CHUNK, GLA_CHUNK), dtype=bool))
    scores = jnp.where(mask, scores, 0.0)
    o = (jnp.einsum('nbhij,nbhje->nbhie', scores, vc)
         + jnp.einsum('nbhid,nbhde->nbhie', qc * jnp.exp(b), s_prev))
    return o.transpose(1, 0, 3, 2, 4).reshape(bn, n, h, dv), s_final


def gla_branch(q_lat, k_lat, v_lat, r_lat, z_lat, q_ctx, k_ctx, v_ctx, r_ctx, z_ctx,
               gate_w, gate_b, norm_g, ctx_out):
    def heads(q, k, v):
        bn, n, _ = q.shape
        qh = q.astype(jnp.float32).reshape(bn, n, GLA_HEADS, GLA_DK) * (GLA_DK ** -0.5)
        kh = k.astype(jnp.float32).reshape(bn, n, GLA_HEADS, GLA_DK)
        vh = v.astype(jnp.float32).reshape(bn, n, GLA_HEADS, GLA_DV)
        return qh, kh, vh

    def log_gate(z, d):
        bn, n, _ = z.shape
        zz = z[..., d * GLA_GATE_RANK:(d + 1) * GLA_GATE_RANK] @ gate_w[d] + gate_b[d]
        return (jax.nn.log_sigmoid(zz.astype(jnp.float32)) / GLA_TAU).reshape(bn, n, GLA_HEADS, GLA_DK)

    ql, kl, vl = heads(q_lat, k_lat, v_lat)
    qc, kc, vc = heads(q_ctx, k_ctx, v_ctx)
    bn = q_ctx.shape[0]
    s_zero = jnp.zeros((bn, GLA_HEADS, GLA_DK, GLA_DV), jnp.float32)
    o_lat = []
    o_ctx = []
    for d in range(2):
        gl, gcx = log_gate(z_lat, d), log_gate(z_ctx, d)
        lat_in = (ql, kl, vl, gl)
        ctx_in = (qc, kc, vc, gcx)
        if d == 1:
            lat_in = tuple(jnp.flip(t, axis=1) for t in lat_in)
            ctx_in = tuple(jnp.flip(t, axis=1) for t in ctx_in)
        oc, s_c = gla_chunked(*ctx_in, s_zero, ctx_out)
        ol, _ = gla_chunked(*lat_in, s_c, True)
        if d == 1:
            ol = jnp.flip(ol, axis=1)
            oc = jnp.flip(oc, axis=1) if ctx_out else None
        o_lat.append(ol)
        o_ctx.append(oc)

    def finish(o, r):
        bn_, n_ = o.shape[0], o.shape[1]
        on = rms_norm(o, norm_g).reshape(bn_, n_, GLA_HEADS * GLA_DV)
        return (on * jax.nn.silu(r.astype(jnp.float32))).astype(r.dtype)

    out_lat = finish(o_lat[0] + o_lat[1], r_lat)
    out_ctx = finish(o_ctx[0] + o_ctx[1], r_ctx) if ctx_out else None
    return out_lat, out_ctx


def rope_rotate(x, ang):
    half = x.shape[-1] // 2
    cos = jnp.cos(ang)[None, :, None, :]
    sin = jnp.sin(ang)[None, :, None, :]
    xf = x.astype(jnp.float32)
    x1, x2 = xf[..., :half], xf[..., half:]
    return jnp.concatenate([x1 * cos - x2 * sin, x2 * cos + x1 * sin], axis=-1).astype(x.dtype)


def axial_rope(x, ang_r, ang_c):
    return jnp.concatenate([rope_rotate(x[..., :ROPE_AXIS_DIM], ang_r),
                            rope_rotate(x[..., ROPE_AXIS_DIM:], ang_c)], axis=-1)


def gqa_attend(q, k, v):
    s = jnp.einsum('bqhgd,bkhd->bhgqk', q, k).astype(jnp.float32) * (HEAD_DIM ** -0.5)
    p = jax.nn.softmax(s, axis=-1)
    return jnp.einsum('bhgqk,bkhd->bqhgd', p.astype(v.dtype), v)


def gqa_branch(q_lat, k_lat, v_lat, q_ctx, k_ctx, v_ctx, q_norm_g, k_norm_g, ang_r, ang_c, ctx_out):
    bn, n, _ = q_lat.shape
    l = q_ctx.shape[1]

    def heads(t, h):
        return t.reshape(t.shape[0], t.shape[1], h, HEAD_DIM)

    ql = axial_rope(rms_norm(heads(q_lat, ATT_HEADS), q_norm_g), ang_r, ang_c)
    kl = axial_rope(rms_norm(heads(k_lat, ATT_KV_HEADS), k_norm_g), ang_r, ang_c)
    kc = rms_norm(heads(k_ctx, ATT_KV_HEADS), k_norm_g)
    vc = heads(v_ctx, ATT_KV_HEADS)
    k_all = jnp.concatenate([kc, kl], axis=1)
    v_all = jnp.concatenate([vc, heads(v_lat, ATT_KV_HEADS)], axis=1)
    nb = n // Q_BLOCK
    qb = ql.reshape(bn, nb, Q_BLOCK, ATT_KV_HEADS, ATT_GROUP, HEAD_DIM).transpose(1, 0, 2, 3, 4, 5)
    ob = lax.map(lambda blk: gqa_attend(blk, k_all, v_all), qb)
    out_lat = ob.transpose(1, 0, 2, 3, 4, 5).reshape(bn, n, ATT_HEADS * HEAD_DIM)
    out_ctx = None
    if ctx_out:
        qc = rms_norm(heads(q_ctx, ATT_HEADS), q_norm_g).reshape(bn, l, ATT_KV_HEADS, ATT_GROUP, HEAD_DIM)
        out_ctx = gqa_attend(qc, kc, vc).reshape(bn, l, ATT_HEADS * HEAD_DIM)
    return out_lat, out_ctx


def merge_branches(y_lru, y_gla, y_att, gate_logits, w_branch, w_out):
    g = jax.nn.sigmoid(gate_logits.astype(jnp.float32)).astype(y_lru.dtype)
    g1, g2, g3 = jnp.split(g, N_BRANCHES, axis=-1)
    m = g1 * (y_lru @ w_branch[0]) + g2 * (y_gla @ w_branch[1]) + g3 * (y_att @ w_branch[2])
    return m @ w_out


def squared_relu_mlp(h, w1, w2):
    return jnp.square(jax.nn.relu(h @ w1)) @ w2


def trunk_layer(x, x_ctx, c_silu, cc_silu, mod_w, mod_b, norm1_g, norm2_g, w_in,
                conv_w, conv_b, lru_a_w, lru_a_b, lru_x_w, lru_x_b, lru_lambda,
                gla_gate_w, gla_gate_b, gla_norm_g, q_norm_g, k_norm_g,
                w_branch, w_out, mlp_w1, mlp_w2, ang_r, ang_c, ctx_out):
    sh1, sc1, ga1, sh2, sc2, ga2 = jnp.split((c_silu @ mod_w + mod_b)[:, None, :], 6, axis=-1)
    sh1c, sc1c, ga1c, sh2c, sc2c, ga2c = jnp.split(cc_silu @ mod_w + mod_b, 6, axis=-1)
    h = modulate(x, norm1_g, sh1, sc1)
    hc = modulate(x_ctx, norm1_g, sh1c, sc1c)
    lx, ly, gq, gk, gv, gr, gz, aq, ak, av, bg = jnp.split(h @ w_in, PROJ_SPLITS, axis=-1)
    lxc, lyc, gqc, gkc, gvc, grc, gzc, aqc, akc, avc, bgc = jnp.split(hc @ w_in, PROJ_SPLITS, axis=-1)
    y1, y1c = rglru_branch(lx, ly, lxc, lyc, conv_w, conv_b, lru_a_w, lru_a_b, lru_x_w, lru_x_b,
                           lru_lambda, ctx_out)
    y2, y2c = gla_branch(gq, gk, gv, gr, gz, gqc, gkc, gvc, grc, gzc, gla_gate_w, gla_gate_b,
                         gla_norm_g, ctx_out)
    y3, y3c = gqa_branch(aq, ak, av, aqc, akc, avc, q_norm_g, k_norm_g, ang_r, ang_c, ctx_out)
    x = x + ga1 * merge_branches(y1, y2, y3, bg, w_branch, w_out)
    x = x + ga2 * squared_relu_mlp(modulate(x, norm2_g, sh2, sc2), mlp_w1, mlp_w2)
    if not ctx_out:
        return x, None
    x_ctx = x_ctx + ga1c * merge_branches(y1c, y2c, y3c, bgc, w_branch, w_out)
    x_ctx = x_ctx + ga2c * squared_relu_mlp(modulate(x_ctx, norm2_g, sh2c, sc2c), mlp_w1, mlp_w2)
    return x, x_ctx


def setup_inputs(seed: int = 0) -> dict:
    key = jax.random.key(seed)
    ks = jax.random.split(key, 32)
    f32 = jnp.float32
    nrm = lambda k, shape, s: jax.random.normal(k, shape, f32) * s
    a8 = jax.random.uniform(ks[14], (DEPTH, 2, LRU_WIDTH), f32, minval=0.9, maxval=0.999)
    s_lam = a8 ** (1.0 / RGLRU_C)
    lru_lambda = jnp.log(s_lam) - jnp.log1p(-s_lam)
    return {
        'x': nrm(ks[0], (BATCH, SEQ, D_MODEL), 1.0),
        'c': nrm(ks[1], (BATCH, D_MODEL), 1.0),
        'ctx': nrm(ks[2], (BATCH, CTX_LEN, D_MODEL), 1.0),
        'c_ctx': nrm(ks[3], (D_MODEL,), 1.0),
        'mod_w': nrm(ks[4], (DEPTH, D_MODEL, 6 * D_MODEL), 0.5 * D_MODEL ** -0.5),
        'mod_b': nrm(ks[5], (DEPTH, 6 * D_MODEL), 0.01),
        'norm1_g': 1.0 + nrm(ks[6], (DEPTH, D_MODEL), 0.02),
        'norm2_g': 1.0 + nrm(ks[7], (DEPTH, D_MODEL), 0.02),
        'w_in': nrm(ks[8], (DEPTH, D_MODEL, D_PROJ), D_MODEL ** -0.5),
        'conv_w': nrm(ks[9], (DEPTH, CONV_WIDTH, LRU_WIDTH), CONV_WIDTH ** -0.5),
        'conv_b': nrm(ks[10], (DEPTH, LRU_WIDTH), 0.01),
        'lru_a_w': nrm(ks[11], (DEPTH, 2, LRU_BLOCKS, LRU_BLOCK_W, LRU_BLOCK_W), LRU_BLOCK_W ** -0.5),
        'lru_a_b': nrm(ks[12], (DEPTH, 2, LRU_WIDTH), 0.01),
        'lru_x_w': nrm(ks[13], (DEPTH, 2, LRU_BLOCKS, LRU_BLOCK_W, LRU_BLOCK_W), LRU_BLOCK_W ** -0.5),
        'lru_x_b': nrm(ks[15], (DEPTH, 2, LRU_WIDTH), 0.01),
        'lru_lambda': lru_lambda,
        'gla_gate_w': nrm(ks[16], (DEPTH, 2, GLA_GATE_RANK, GLA_HEADS * GLA_DK), GLA_GATE_RANK ** -0.5),
        'gla_gate_b': nrm(ks[17], (DEPTH, 2, GLA_HEADS * GLA_DK), 0.1),
        'gla_norm_g': 1.0 + nrm(ks[18], (DEPTH, GLA_DV), 0.02),
        'q_norm_g': 1.0 + nrm(ks[19], (DEPTH, HEAD_DIM), 0.02),
        'k_norm_g': 1.0 + nrm(ks[20], (DEPTH, HEAD_DIM), 0.02),
        'w_branch': nrm(ks[21], (DEPTH, N_BRANCHES, BRANCH_WIDTH, D_MODEL), BRANCH_WIDTH ** -0.5),
        'w_out': nrm(ks[22], (DEPTH, D_MODEL, D_MODEL), D_MODEL ** -0.5),
        'mlp_w1': nrm(ks[23], (DEPTH, D_MODEL, D_FF), D_MODEL ** -0.5),
        'mlp_w2': nrm(ks[24], (DEPTH, D_FF, D_MODEL), D_FF ** -0.5),
    }


def reference(x, c, ctx, c_ctx, mod_w, mod_b, norm1_g, norm2_g, w_in, conv_w, conv_b,
              lru_a_w, lru_a_b, lru_x_w, lru_x_b, lru_lambda, gla_gate_w, gla_gate_b, gla_norm_g,
              q_norm_g, k_norm_g, w_branch, w_out, mlp_w1, mlp_w2):
    n_lat = x.shape[1]
    rows = n_lat // GRID_W
    row_pos = jnp.repeat(jnp.arange(rows), GRID_W).astype(jnp.float32)
    col_pos = jnp.tile(jnp.arange(GRID_W), rows).astype(jnp.float32)
    inv_freq = ROPE_THETA ** (-jnp.arange(0, ROPE_AXIS_DIM, 2, dtype=jnp.float32) / ROPE_AXIS_DIM)
    ang_r = row_pos[:, None] * inv_freq[None, :]
    ang_c = col_pos[:, None] * inv_freq[None, :]
    c_silu = jax.nn.silu(c)
    cc_silu = jax.nn.silu(c_ctx)
    x_ctx = ctx
    for l in range(DEPTH):
        x, x_ctx = trunk_layer(x, x_ctx, c_silu, cc_silu, mod_w[l], mod_b[l], norm1_g[l], norm2_g[l], w_in[l],
                               conv_w[l], conv_b[l], lru_a_w[l], lru_a_b[l], lru_x_w[l], lru_x_b[l],
                               lru_lambda[l], gla_gate_w[l], gla_gate_b[l], gla_norm_g[l],
                               q_norm_g[l], k_norm_g[l], w_branch[l], w_out[l], mlp_w1[l], mlp_w2[l],
                               ang_r, ang_c, l < DEPTH - 1)
    return x
```

```python
import contextlib
import numpy as np
import ml_dtypes
import concourse.bass as bass
import concourse.mybir as mybir
from concourse.bass_utils import run_bass_kernel_spmd

F32 = mybir.dt.float32
BF16 = mybir.dt.bfloat16
AF = mybir.ActivationFunctionType
ALU = mybir.AluOpType

D = 1024
DPROJ = 9760
EPS = 1e-6


class Chan:
    def __init__(self, sem):
        self.sem = sem
        self.count = 0


class Buf:
    def __init__(self, t, k):
        self.t = t
        self.k = k


class Sched:
    ENG = ['pe', 'act', 'dve', 'pool', 'sp']

    def __init__(self, nc, stack, nsem=96):
        self.nc = nc
        self.stack = stack
        self.free = [stack.enter_context(nc.semaphore("s%d" % i)) for i in range(nsem)]
        self.q = {e: [] for e in self.ENG}
        self.echan = {e: Chan(self.free.pop()) for e in ['pe', 'act', 'dve', 'pool']}
        self.chans = []
        self.chanpool = {'hw': [], 'sw': []}
        self.scoped = []
        self.seen = {e: {} for e in self.ENG}
        self.res = {}
        self.nt = 0

    def newchan(self, kind='hw', barrier=True, scoped=True):
        if barrier and scoped and self.chanpool[kind]:
            c = self.chanpool[kind].pop()
        else:
            c = Chan(self.free.pop())
            c.kind = kind
            if barrier:
                self.chans.append(c)
        if barrier and scoped:
            self.scoped.append(c)
        return c

    def sb(self, shape, dtype, stack=None):
        self.nt += 1
        st = stack or self.stack
        t = st.enter_context(self.nc.sbuf_tensor("t%d" % self.nt, list(shape), dtype))
        return Buf(t, "b%d" % self.nt)

    def ps(self, shape, dtype):
        self.nt += 1
        t = self.stack.enter_context(self.nc.psum_tensor("p%d" % self.nt, list(shape), dtype))
        return Buf(t, "p%d" % self.nt)

    def _deps(self, eng, reads, writes, skip=None):
        need = {}

        def add(d):
            for cid, (ch, v) in d.items():
                if skip is not None and ch is skip:
                    continue
                if self.seen[eng].get(cid, 0) >= v:
                    continue
                if need.get(cid, (None, 0))[1] < v:
                    need[cid] = (ch, v)
        for k in reads:
            r = self.res.get(k)
            if r:
                add(r['w'])
        for k in writes:
            r = self.res.get(k)
            if r:
                add(r['w'])
                add(r['r'])
        for cid, (ch, v) in need.items():
            self.seen[eng][cid] = v
            self.q[eng].append(('wait', ch.sem, v))

    def _commit(self, ev, reads, writes):
        ch, v = ev
        for k in reads:
            self.res.setdefault(k, {'w': {}, 'r': {}})['r'][id(ch)] = ev
        for k in writes:
            self.res[k] = {'w': {id(ch): ev}, 'r': {}}

    def op(self, eng, fn, reads=(), writes=()):
        skip = self.echan['pe'] if eng == 'pe' else None
        self._deps(eng, reads, writes, skip=skip)
        ch = self.echan[eng]
        ch.count += 1
        self.q[eng].append(('op', fn, ch.sem))
        self._commit((ch, ch.count), reads, writes)

    def dma(self, q, out, in_, reads, writes, chan):
        assert chan.kind == ('sw' if q == 'pool' else 'hw'), (q, chan.kind)
        self._deps(q, reads, writes)
        chan.count += 16
        self.q[q].append(('dma', out, in_, chan.sem))
        self._commit((chan, chan.count), reads, writes)

    def barrier(self):
        allc = list(self.echan.values()) + self.chans
        for e in self.ENG:
            for ch in allc:
                if ch.count > self.seen[e].get(id(ch), 0):
                    self.seen[e][id(ch)] = ch.count
                    self.q[e].append(('wait', ch.sem, ch.count))
        self.res = {k: v for k, v in self.res.items() if isinstance(k, tuple)}
        for c in self.scoped:
            self.chanpool[c.kind].append(c)
        self.scoped = []

    def _replay(self, e, name):
        for it in self.q[name]:
            if it[0] == 'wait':
                e.wait_ge(it[1], it[2])
            elif it[0] == 'op':
                it[1](e).then_inc(it[2], 1)
            else:
                e.dma_start(out=it[1], in_=it[2]).then_inc(it[3], 16)

    def emit(self):
        with self.nc.Block() as block:
            @block.sync
            def _(e):
                self._replay(e, 'sp')

            @block.tensor
            def _(e):
                self._replay(e, 'pe')

            @block.scalar
            def _(e):
                self._replay(e, 'act')

            @block.vector
            def _(e):
                self._replay(e, 'dve')

            @block.gpsimd
            def _(e):
                self._replay(e, 'pool')


PV_N1G, PV_N2G, PV_MODB, PV_CW, PV_CB, PV_AB, PV_XB, PV_LAM, PV_GB, PV_QG, PV_KG, PV_W = \
    0, 8, 16, 48, 80, 88, 104, 120, 136, 144, 145, 146
PBC_W = 2048 + 256

IN_BLOCKS = ([('lx', 0 + 512 * i, 512) for i in range(2)] + [('ly', 1024 + 512 * i, 512) for i in range(2)] +
             [('gq', 2048, 512), ('gk', 2560, 512)] + [('gv', 3072 + 512 * i, 512) for i in range(2)] +
             [('gr', 4096 + 512 * i, 512) for i in range(2)] + [('gz', 5120, 32)] +
             [('aq', 5152 + 512 * i, 512) for i in range(2)] + [('ak', 6176, 256), ('av', 6432, 256)] +
             [('bg', 6688 + 512 * i, 512) for i in range(6)])


def build(N, LC, DEPTH, debug=False):
    S_ = LC + N
    NT = N // 512
    tiles = [(0, LC)] + [(LC + 512 * i, 512) for i in range(NT)]
    NCH = S_ // 128
    nc = bass.Bass("TRN2", target_bir_lowering=False)

    def din(name, shape, dt=F32):
        return nc.dram_tensor(name, list(shape), dt, kind="ExternalInput").ap()

    def dscr(name, shape, dt):
        return nc.dram_tensor(name, list(shape), dt, kind="ExternalOutput" if debug else "Internal").ap()

    xin = din("xin", [S_, D])
    ccT = din("ccT", [128, 8, 2])
    pvec = din("pvec", [DEPTH, 128, PV_W])
    pbc = din("pbc", [DEPTH, 128, PBC_W])
    mod_w = din("mod_w", [DEPTH, D, 6 * D])
    w_in = din("w_in", [DEPTH, D, DPROJ])
    lru_a_w = din("lru_a_w", [DEPTH, 2, 16, 64, 64])
    lru_x_w = din("lru_x_w", [DEPTH, 2, 16, 64, 64])
    gla_gate_w = din("gla_gate_w", [DEPTH, 2, 16, 512])
    w_branch = din("w_branch", [DEPTH, 3, D, D])
    w_out = din("w_out", [DEPTH, D, D])
    mlp_w1 = din("mlp_w1", [DEPTH, D, 4 * D])
    mlp_w2 = din("mlp_w2", [DEPTH, 4 * D, D])
    cosT = din("cosT", [128, N])
    sinT = din("sinT", [128, N])
    consts = din("consts", [128, 6, 128])
    rmask = din("rmask", [128, 2, 512])
    out = nc.dram_tensor("out", [N, D], F32, kind="ExternalOutput").ap()

    xres = dscr("xres", [S_, D], F32)
    lxT = dscr("lxT", [D, S_], F32)
    gyT = dscr("gyT", [D, S_], F32)
    gqT = dscr("gqT", [512, S_], F32)
    gkT = dscr("gkT", [512, S_], F32)
    zT = dscr("zT", [2, 16, S_], BF16)
    gvD = dscr("gvD", [S_, D], BF16)
    srD = dscr("srD", [S_, D], F32)
    qT = dscr("qT", [D, S_], BF16)
    kT = dscr("kT", [256, S_], BF16)
    avD = dscr("avD", [S_, 256], BF16)
    bgT = dscr("bgT", [3 * D, S_], F32)
    y1T = dscr("y1T", [D, S_], BF16)
    y2T = dscr("y2T", [D, S_], BF16)
    y3T = dscr("y3T", [D, S_], BF16)
    w_in16 = dscr("w_in16", [DEPTH, D, DPROJ], BF16)
    w_br16 = dscr("w_br16", [DEPTH, 3, D, D], BF16)
    w_out16 = dscr("w_out16", [DEPTH, D, D], BF16)
    w116 = dscr("w116", [DEPTH, D, 4 * D], BF16)
    w216 = dscr("w216", [DEPTH, 4 * D, D], BF16)

    with contextlib.ExitStack() as st:
        S = Sched(nc, st)
        op = S.op

        def act(out_, in_, func, r, w, bias=None, scale=None, accum=None):
            kw = {}
            if bias is not None:
                kw['bias'] = bias
            if scale is not None:
                kw['scale'] = scale
            if accum is not None:
                kw['accum_out'] = accum
            op('act', lambda e: e.activation(out=out_, in_=in_, func=func, **kw), r, w)

        def tt(eng, out_, a, b, o, r, w):
            op(eng, lambda e: e.tensor_tensor(out=out_, in0=a, in1=b, op=o), r, w)

        def ts(eng, out_, a, s1, s2, o0, o1, r, w):
            if s2 is None:
                op(eng, lambda e: e.tensor_scalar(out=out_, in0=a, scalar1=s1, scalar2=None, op0=o0), r, w)
            else:
                op(eng, lambda e: e.tensor_scalar(out=out_, in0=a, scalar1=s1, scalar2=s2, op0=o0, op1=o1), r, w)

        def stt(out_, a, sc, b, o0, o1, r, w):
            op('dve', lambda e: e.scalar_tensor_tensor(out=out_, in0=a, scalar=sc, in1=b, op0=o0, op1=o1), r, w)

        def cp(eng, out_, in_, r, w):
            if eng == 'act':
                op('act', lambda e: e.copy(out=out_, in_=in_), r, w)
            else:
                op(eng, lambda e: e.tensor_copy(out=out_, in_=in_), r, w)

        def mm(out_, lhsT, rhs, start, stop, r, w):
            op('pe', lambda e: e.matmul(out_, lhsT=lhsT, rhs=rhs, start=start, stop=stop), r, w)

        def tr(out_, in_, ident, r, w):
            op('pe', lambda e: e.transpose(out_, in_, ident), r, w)

        def scan(out_, d0, d1, init, r, w):
            op('dve', lambda e: e.tensor_tensor_scan(out=out_, data0=d0, data1=d1, initial=init,
                                                     op0=ALU.mult, op1=ALU.add), r, w)

        class Rot:
            def __init__(self, n, shape, dt, stack, kind='hw'):
                self.b = [S.sb(shape, dt, stack) for _ in range(n)]
                self.c = [S.newchan('hw' if kind == 'both' else kind) for _ in range(n)]
                self.c2 = [S.newchan('sw') for _ in range(n)] if kind == 'both' else None
                self.i = 0

            def store_chan(self):
                return self.c2[(self.i - 1) % len(self.b)]

            def next(self):
                j = self.i % len(self.b)
                self.i += 1
                return self.b[j], self.c[j]

        PS = [S.ps([128, 512], F32) for _ in range(7)]
        PB = S.ps([128, 1024], BF16)
        cst = S.sb([128, 6, 128], F32)
        ident = cst.t[:, 0, :]
        ones_s = cst.t[:, 1, :]
        rotT = cst.t[:, 2, :]
        cstb = S.sb([128, 6, 128], BF16)
        identb = cstb.t[:, 0, :]
        onesb = cstb.t[:, 5, :]
        rm = S.sb([128, 2, 512], F32)
        pv = S.sb([128, DEPTH, PV_W], F32)
        eps_t = S.sb([128, 1], F32)
        one_t = S.sb([128, 1], F32)
        cs = S.sb([128, 8, 2], F32)
        csB = S.sb([128, 16, 128], F32)
        S.dma('sp', cst.t[:], consts, [], [cst.k], S.newchan(scoped=False))
        S.dma('sp', rm.t[:], rmask, [], [rm.k], S.newchan(scoped=False))
        S.dma('sp', pv.t[:], pvec.rearrange("l p w -> p l w"), [], [pv.k], S.newchan(scoped=False))
        S.dma('sp', cs.t[:], ccT, [], [cs.k], S.newchan(scoped=False))
        op('pool', lambda e: e.memset(eps_t.t[:], EPS), [], [eps_t.k])
        op('pool', lambda e: e.memset(one_t.t[:], 1.0), [], [one_t.k])
        cp('dve', cstb.t[:], cst.t[:], [cst.k], [cstb.k])
        act(cs.t[:], cs.t[:], AF.Silu, [cs.k], [cs.k])
        cp('dve', csB.t[:], cs.t[:].rearrange("p c j -> p (c j)").unsqueeze(2).broadcast_to([128, 16, 128]),
           [cs.k], [csB.k])

        wcs = [S.newchan('sw', barrier=False), S.newchan('sw', barrier=False)]

        def cast_layer(l):
            k = ('W16', l)
            wc = wcs[l % 2]
            for r0 in range(0, D, 256):
                S.dma('pool', w_in16[l, r0:r0 + 256, :], w_in[l, r0:r0 + 256, :], [], [], wc)
                S.dma('pool', w116[l, r0:r0 + 256, :], mlp_w1[l, r0:r0 + 256, :], [], [], wc)
                S.dma('pool', w_out16[l, r0:r0 + 256, :], w_out[l, r0:r0 + 256, :], [], [], wc)
                for i in range(3):
                    S.dma('pool', w_br16[l, i, r0:r0 + 256, :], w_branch[l, i, r0:r0 + 256, :], [], [], wc)
            for r0 in range(0, 4 * D, 512):
                S.dma('pool', w216[l, r0:r0 + 512, :], mlp_w2[l, r0:r0 + 512, :], [], [], wc)
            S.res[k] = {'w': {id(wc): (wc, wc.count)}, 'r': {}}

        cast_layer(0)

        modT = S.sb([128, 32, 2], F32)
        gm = S.sb([128, 2, 8, 2], F32)
        gaTM = S.sb([128, 2, 2, D], F32)
        pbc_t = S.sb([128, PBC_W], F32)
        lrc = S.sb([128, 2, 2, 8], F32)
        negb = S.sb([128, 8], F32)
        qkg = S.sb([128, 2], F32)
        bd = S.sb([128, 2, 2, 8, 128], BF16)
        gw = S.sb([16, 2, 512], BF16)
        op('pool', lambda e: e.memset(bd.t[:], 0.0), [], [bd.k])

        def norm_front(ph, xt, nsub, j, which, hT, xn, ssq, rstd, junk, pbank):
            for s in range(nsub):
                act(junk.t[:], xt.t[:, s, :], AF.Square, [xt.k], [junk.k, ssq.k], accum=ssq.t[:, s:s + 1])
            act(rstd.t[:, 0:nsub], ssq.t[:, 0:nsub], AF.Sqrt, [ssq.k, eps_t.k], [rstd.k], bias=eps_t.t[:], scale=1.0 / D)
            op('dve', lambda e, t_=rstd.t[:, 0:nsub]: e.reciprocal(out=t_, in_=t_), [rstd.k], [rstd.k])
            for s in range(nsub):
                ts('pool', xn.t[:, s, :], xt.t[:, s, :], rstd.t[:, s:s + 1], None, ALU.mult, None, [xt.k, rstd.k], [xn.k])
            for c in range(8):
                pb = pbank[c % len(pbank)]
                for s in range(nsub):
                    tr(pb.t[:, s * 128:(s + 1) * 128], xn.t[:, s, c * 128:(c + 1) * 128], ident, [xn.k, cst.k], [pb.k])
                act(hT.t[:, c, 0:nsub * 128], pb.t[:, 0:nsub * 128], AF.Identity, [pb.k, gm.k, modT.k], [hT.k],
                    bias=modT.t[:, (0 if which == 0 else 16) + c, j:j + 1], scale=gm.t[:, which, c, j:j + 1])

        for l in range(DEPTH):
            xsrc = xin if l == 0 else xres
            last = (l == DEPTH - 1)
            with contextlib.ExitStack() as ph:
                mwb = Rot(2, [128, 8, 512], F32, ph)
                S.dma('sp', pbc_t.t[:], pbc[l], [], [pbc_t.k], S.newchan())
                S.dma('pool', gw.t[:], gla_gate_w[l].rearrange("d r n -> r d n"), [], [gw.k], S.newchan('sw'))
                bdc = S.newchan('sw')
                for d in range(2):
                    for gi, wsrc in enumerate((lru_a_w, lru_x_w)):
                        for half in range(2):
                            src = wsrc[l, d].rearrange("(c two) i j -> two i c j", two=2)[half]
                            S.dma('pool', bd.t[half * 64:(half + 1) * 64, d, gi, :, half * 64:(half + 1) * 64], src,
                                  [], [bd.k], bdc)
                pm = PS[0]
                fm_cols = [0, 1024, 3072, 4096]
                for vi, c0 in enumerate(fm_cols):
                    for hb in range(2):
                        wb_, wch = mwb.next()
                        S.dma('sp', wb_.t[:], mod_w[l, :, c0 + hb * 512:c0 + (hb + 1) * 512].rearrange("(c p) n -> p c n", p=128),
                              [], [wb_.k], wch)
                        for g in range(4):
                            idx = vi * 8 + hb * 4 + g
                            for kc in range(8):
                                mm(pm.t[:, 2 * idx:2 * idx + 2], wb_.t[:, kc, g * 128:(g + 1) * 128], cs.t[:, kc, :],
                                   kc == 0, kc == 7, [wb_.k, cs.k], [pm.k])
                for j in range(2):
                    tt('dve', modT.t[:, :, j], pm.t[:, 0:64].rearrange("p (i j) -> p i j", j=2)[:, :, j],
                       pv.t[:, l, PV_MODB:PV_MODB + 32], ALU.add, [pm.k, pv.k], [modT.k])
                for which, (sc0, g0) in enumerate(((8, PV_N1G), (24, PV_N2G))):
                    for j in range(2):
                        stt(gm.t[:, which, :, j], modT.t[:, sc0:sc0 + 8, j], 1.0, pv.t[:, l, g0:g0 + 8], ALU.add, ALU.mult,
                            [modT.k, pv.k], [gm.k])
                for which, c0 in enumerate((2048, 5120)):
                    for hb in range(2):
                        wb_, wch = mwb.next()
                        S.dma('sp', wb_.t[:], mod_w[l, :, c0 + hb * 512:c0 + (hb + 1) * 512].rearrange("(c p) n -> p c n", p=128),
                              [], [wb_.k], wch)
                        for j in range(2):
                            pg = PS[1 + j]
                            for kc in range(8):
                                mm(pg.t[:], csB.t[:, kc * 2 + j, :], wb_.t[:, kc, :], kc == 0, kc == 7, [wb_.k, csB.k], [pg.k])
                            tt('dve', gaTM.t[:, j, which, hb * 512:(hb + 1) * 512], pg.t[:],
                               pbc_t.t[:, which * 1024 + hb * 512:which * 1024 + (hb + 1) * 512], ALU.add,
                               [pg.k, pbc_t.k], [gaTM.k])
                lam = pv.t[:, l, PV_LAM:PV_LAM + 16].rearrange("p (d c) -> p d c", d=2)
                act(lrc.t[:, :, 0, :], lam, AF.Exp, [pv.k], [lrc.k], scale=-1.0)
                act(lrc.t[:, :, 0, :], lrc.t[:, :, 0, :], AF.Ln, [lrc.k, one_t.k], [lrc.k], bias=one_t.t[:])
                ts('dve', lrc.t[:, :, 1, :], lrc.t[:, :, 0, :], -16.0, None, ALU.mult, None, [lrc.k], [lrc.k])
                ts('dve', lrc.t[:, :, 0, :], lrc.t[:, :, 0, :], -8.0, None, ALU.mult, None, [lrc.k], [lrc.k])
                ts('dve', negb.t[:], pv.t[:, l, PV_GB:PV_GB + 8], -1.0, None, ALU.mult, None, [pv.k], [negb.k])
                ts('dve', qkg.t[:, 0:1], pv.t[:, l, PV_QG:PV_QG + 1], 128.0 ** -0.5, None, ALU.mult, None, [pv.k], [qkg.k])
                cp('dve', qkg.t[:, 1:2], pv.t[:, l, PV_KG:PV_KG + 1], [pv.k], [qkg.k])
                S.barrier()

            with contextlib.ExitStack() as ph:
                xts = Rot(2, [128, 4, D], F32, ph)
                xn = S.sb([128, 4, D], F32, ph)
                hTs = [S.sb([128, 8, 512], BF16, ph) for _ in range(2)]
                ssq = S.sb([128, 4], F32, ph)
                rstd = S.sb([128, 4], F32, ph)
                junk = S.sb([128, D], F32, ph)
                wbl = Rot(4, [128, 8, 512], BF16, ph)
                stf = Rot(4, [128, 512], F32, ph, 'sw')
                stb = Rot(4, [128, 512], BF16, ph, 'sw')
                cst_ = Rot(2, [128, 2, 512], F32, ph)
                qtmp = [[S.sb([128, 512], F32, ph) for _ in range(4)] for _ in range(2)]
                qi = 0
                pbi = 0
                for ti, (t0, tn) in enumerate(tiles):
                    nsub = tn // 128
                    j = 1 if ti == 0 else 0
                    xt, xch = xts.next()
                    S.dma('sp', xt.t[:, 0:nsub, :], xsrc[t0:t0 + tn, :].rearrange("(s p) d -> p s d", p=128), [], [xt.k], xch)
                    if ti > 0:
                        cs_, cch = cst_.next()
                        S.dma('sp', cs_.t[:, 0, :], cosT[:, t0 - LC:t0 - LC + tn], [], [cs_.k], cch)
                        S.dma('sp', cs_.t[:, 1, :], sinT[:, t0 - LC:t0 - LC + tn], [], [cs_.k], cch)
                    hT = hTs[ti % 2]
                    norm_front(ph, xt, nsub, j, 0, hT, xn, ssq, rstd, junk, [PS[0], PS[1]])
                    for (kind, c0, ncols) in IN_BLOCKS:
                        wb_, wch = wbl.next()
                        S.dma('sp', wb_.t[:, :, 0:ncols], w_in16[l, :, c0:c0 + ncols].rearrange("(c p) n -> p c n", p=128),
                              [('W16', l)], [wb_.k], wch)
                        if kind in ('gv', 'gr', 'av'):
                            for s in range(nsub):
                                pb = PS[2 + pbi % 5]
                                pbi += 1
                                for kc in range(8):
                                    mm(pb.t[:, 0:ncols], hT.t[:, kc, s * 128:(s + 1) * 128], wb_.t[:, kc, 0:ncols],
                                       kc == 0, kc == 7, [hT.k, wb_.k], [pb.k])
                                r0 = t0 + s * 128
                                if kind == 'gv':
                                    sg, sc_ = stb.next()
                                    cp('dve', sg.t[:, 0:ncols], pb.t[:, 0:ncols], [pb.k], [sg.k])
                                    S.dma('pool', gvD[r0:r0 + 128, c0 - 3072:c0 - 3072 + ncols], sg.t[:, 0:ncols], [sg.k], [], sc_)
                                elif kind == 'gr':
                                    sg, sc_ = stf.next()
                                    act(sg.t[:, 0:ncols], pb.t[:, 0:ncols], AF.Silu, [pb.k], [sg.k])
                                    S.dma('pool', srD[r0:r0 + 128, c0 - 4096:c0 - 4096 + ncols], sg.t[:, 0:ncols], [sg.k], [], sc_)
                                else:
                                    sg, sc_ = stb.next()
                                    cp('dve', sg.t[:, 0:ncols], pb.t[:, 0:ncols], [pb.k], [sg.k])
                                    S.dma('pool', avD[r0:r0 + 128, :], sg.t[:, 0:ncols], [sg.k], [], sc_)
                            continue
                        gwid = 16 if kind == 'gz' else 128
                        for g in range(ncols // gwid):
                            pb = PS[2 + pbi % 5]
                            pbi += 1
                            for kc in range(8):
                                mm(pb.t[0:gwid, 0:tn], wb_.t[:, kc, g * gwid:(g + 1) * gwid], hT.t[:, kc, 0:tn],
                                   kc == 0, kc == 7, [hT.k, wb_.k], [pb.k])
                            f0 = c0 + g * gwid
                            if kind == 'lx':
                                sg, sc_ = stf.next()
                                cp('dve', sg.t[:, 0:tn], pb.t[:, 0:tn], [pb.k], [sg.k])
                                S.dma('pool', lxT[f0:f0 + 128, t0:t0 + tn], sg.t[:, 0:tn], [sg.k], [], sc_)
                            elif kind == 'ly':
                                sg, sc_ = stf.next()
                                act(sg.t[:, 0:tn], pb.t[:, 0:tn], AF.Gelu_apprx_tanh, [pb.k], [sg.k])
                                S.dma('pool', gyT[f0 - 1024:f0 - 1024 + 128, t0:t0 + tn], sg.t[:, 0:tn], [sg.k], [], sc_)
                            elif kind == 'gq':
                                sg, sc_ = stf.next()
                                ts('dve', sg.t[:, 0:tn], pb.t[:, 0:tn], 128.0 ** -0.5, None, ALU.mult, None, [pb.k], [sg.k])
                                S.dma('pool', gqT[f0 - 2048:f0 - 2048 + 128, t0:t0 + tn], sg.t[:, 0:tn], [sg.k], [], sc_)
                            elif kind == 'gk':
                                sg, sc_ = stf.next()
                                cp('dve', sg.t[:, 0:tn], pb.t[:, 0:tn], [pb.k], [sg.k])
                                S.dma('pool', gkT[f0 - 2560:f0 - 2560 + 128, t0:t0 + tn], sg.t[:, 0:tn], [sg.k], [], sc_)
                            elif kind == 'gz':
                                sg, sc_ = stb.next()
                                cp('dve', sg.t[0:16, 0:tn], pb.t[0:16, 0:tn], [pb.k], [sg.k])
                                S.dma('pool', zT[g, :, t0:t0 + tn], sg.t[0:16, 0:tn], [sg.k], [], sc_)
                            elif kind == 'bg':
                                sg, sc_ = stf.next()
                                act(sg.t[:, 0:tn], pb.t[:, 0:tn], AF.Sigmoid, [pb.k], [sg.k])
                                S.dma('pool', bgT[f0 - 6688:f0 - 6688 + 128, t0:t0 + tn], sg.t[:, 0:tn], [sg.k], [], sc_)
                            else:
                                isq = kind == 'aq'
                                sq, rs, xq, t1 = qtmp[qi % 2]
                                qi += 1
                                act(sq.t[:, 0:tn], pb.t[:, 0:tn], AF.Square, [pb.k], [sq.k])
                                pm2 = PS[0]
                                mm(pm2.t[:, 0:tn], ones_s, sq.t[:, 0:tn], True, True, [cst.k, sq.k], [pm2.k])
                                act(rs.t[:, 0:tn], pm2.t[:, 0:tn], AF.Sqrt, [pm2.k, eps_t.k], [rs.k], bias=eps_t.t[:])
                                op('dve', lambda e, o_=rs.t[:, 0:tn]: e.reciprocal(out=o_, in_=o_), [rs.k], [rs.k])
                                sg, sc_ = stb.next()
                                gcol = qkg.t[:, 0:1] if isq else qkg.t[:, 1:2]
                                if ti == 0:
                                    stt(sg.t[:, 0:tn], pb.t[:, 0:tn], gcol, rs.t[:, 0:tn], ALU.mult, ALU.mult,
                                        [pb.k, qkg.k, rs.k], [sg.k])
                                else:
                                    stt(xq.t[:, 0:tn], pb.t[:, 0:tn], gcol, rs.t[:, 0:tn], ALU.mult, ALU.mult,
                                        [pb.k, qkg.k, rs.k], [xq.k])
                                    pm3 = PS[1]
                                    mm(pm3.t[:, 0:tn], rotT, xq.t[:, 0:tn], True, True, [cst.k, xq.k], [pm3.k])
                                    tt('pool', t1.t[:, 0:tn], xq.t[:, 0:tn], cs_.t[:, 0, 0:tn], ALU.mult, [xq.k, cs_.k], [t1.k])
                                    tt('dve', sq.t[:, 0:tn], pm3.t[:, 0:tn], cs_.t[:, 1, 0:tn], ALU.mult, [pm3.k, cs_.k], [sq.k])
                                    tt('dve', sg.t[:, 0:tn], t1.t[:, 0:tn], sq.t[:, 0:tn], ALU.add, [t1.k, sq.k], [sg.k])
                                if isq:
                                    S.dma('pool', qT[f0 - 5152:f0 - 5152 + 128, t0:t0 + tn], sg.t[:, 0:tn], [sg.k], [], sc_)
                                else:
                                    S.dma('pool', kT[f0 - 6176:f0 - 6176 + 128, t0:t0 + tn], sg.t[:, 0:tn], [sg.k], [], sc_)
                if l + 1 < DEPTH:
                    cast_layer(l + 1)
                S.barrier()

            with contextlib.ExitStack() as ph:
                OC, OL = 1, LC + 4
                lxp = S.sb([128, LC + 4 + N + 2], F32, ph)
                xc = S.sb([128, S_], F32, ph)
                xcb = S.sb([128, S_], BF16, ph)
                hf = S.sb([128, S_], F32, ph)
                tmp = [[S.sb([128, 512], F32, ph) for _ in range(7)] for _ in range(2)]
                gys = Rot(2, [128, 512], F32, ph)
                ysb = Rot(2, [128, 512], BF16, ph, 'sw')
                lch = S.newchan()
                op('pool', lambda e, t_=lxp.t[:]: e.memset(t_, 0.0), [], [lxp.k])
                cwv = pv.t[:, l, PV_CW:PV_CW + 32].rearrange("p (c k) -> p c k", k=4)
                it = 0
                for cg in range(8):
                    S.dma('sp', lxp.t[:, OC:OC + LC], lxT[cg * 128:(cg + 1) * 128, 0:LC], [], [lxp.k], lch)
                    for n0 in range(0, N, 2048):
                        n1 = min(N, n0 + 2048)
                        S.dma('sp', lxp.t[:, OL + n0:OL + n1], lxT[cg * 128:(cg + 1) * 128, LC + n0:LC + n1], [], [lxp.k], lch)
                    for (so, do, ln) in ((OC, 0, LC), (OL, LC, N)):
                        for n0 in range(0, ln, 2048):
                            n1 = min(ln, n0 + 2048)
                            o_ = xc.t[:, do + n0:do + n1]
                            ts('dve', o_, lxp.t[:, so + n0 - 1:so + n1 - 1], cwv[:, cg, 0:1], pv.t[:, l, PV_CB + cg:PV_CB + cg + 1],
                               ALU.mult, ALU.add, [lxp.k, pv.k], [xc.k])
                            for k in range(1, 4):
                                stt(o_, lxp.t[:, so + n0 - 1 + k:so + n1 - 1 + k], cwv[:, cg, k:k + 1], o_, ALU.mult, ALU.add,
                                    [lxp.k, pv.k, xc.k], [xc.k])
                            cp('pool', xcb.t[:, do + n0:do + n1], o_, [xc.k], [xcb.k])
                    prev_hb = None
                    for d in range(2):
                        order = list(range(len(tiles))) if d == 0 else [0] + list(range(len(tiles) - 1, 0, -1))
                        for oi, ti in enumerate(order):
                            t0, tn = tiles[ti]
                            r_, a_, a2_, s_, i_, u_, hb_ = tmp[it % 2]
                            it += 1
                            pa, px = PS[(it % 3) * 2], PS[(it % 3) * 2 + 1]
                            mm(pa.t[:, 0:tn], bd.t[:, d, 0, cg, :], xcb.t[:, t0:t0 + tn], True, True, [bd.k, xcb.k], [pa.k])
                            mm(px.t[:, 0:tn], bd.t[:, d, 1, cg, :], xcb.t[:, t0:t0 + tn], True, True, [bd.k, xcb.k], [px.k])
                            act(r_.t[:, 0:tn], pa.t[:, 0:tn], AF.Sigmoid, [pa.k, pv.k], [r_.k],
                                bias=pv.t[:, l, PV_AB + d * 8 + cg:PV_AB + d * 8 + cg + 1])
                            act(i_.t[:, 0:tn], px.t[:, 0:tn], AF.Sigmoid, [px.k, pv.k], [i_.k],
                                bias=pv.t[:, l, PV_XB + d * 8 + cg:PV_XB + d * 8 + cg + 1])
                            act(a_.t[:, 0:tn], r_.t[:, 0:tn], AF.Exp, [r_.k, lrc.k], [a_.k], scale=lrc.t[:, d, 0, cg:cg + 1])
                            act(a2_.t[:, 0:tn], r_.t[:, 0:tn], AF.Exp, [r_.k, lrc.k], [a2_.k], scale=lrc.t[:, d, 1, cg:cg + 1])
                            act(s_.t[:, 0:tn], a2_.t[:, 0:tn], AF.Sqrt, [a2_.k, one_t.k], [s_.k], bias=one_t.t[:], scale=-1.0)
                            tt('pool', u_.t[:, 0:tn], i_.t[:, 0:tn], xc.t[:, t0:t0 + tn], ALU.mult, [i_.k, xc.k], [u_.k])
                            tt('pool', u_.t[:, 0:tn], u_.t[:, 0:tn], s_.t[:, 0:tn], ALU.mult, [u_.k, s_.k], [u_.k])
                            if d == 0:
                                init = 0.0 if ti == 0 else hf.t[:, t0 - 1:t0]
                                scan(hf.t[:, t0:t0 + tn], a_.t[:, 0:tn], u_.t[:, 0:tn], init, [a_.k, u_.k, hf.k], [hf.k])
                            else:
                                init = 0.0 if oi == 0 else prev_hb.t[:, 0:1]
                                rk = [a_.k, u_.k] + ([prev_hb.k] if oi > 0 else [])
                                scan(hb_.t[:, 0:tn][:, ::-1], a_.t[:, 0:tn][:, ::-1], u_.t[:, 0:tn][:, ::-1], init, rk, [hb_.k])
                                prev_hb = hb_
                                if not (last and ti == 0):
                                    gy, gch = gys.next()
                                    S.dma('sp', gy.t[:, 0:tn], gyT[cg * 128:(cg + 1) * 128, t0:t0 + tn], [], [gy.k], gch)
                                    tt('dve', r_.t[:, 0:tn], hb_.t[:, 0:tn], hf.t[:, t0:t0 + tn], ALU.add, [hb_.k, hf.k], [r_.k])
                                    yb, ych = ysb.next()
                                    tt('dve', yb.t[:, 0:tn], r_.t[:, 0:tn], gy.t[:, 0:tn], ALU.mult, [r_.k, gy.k], [yb.k])
                                    S.dma('pool', y1T[cg * 128:(cg + 1) * 128, t0:t0 + tn], yb.t[:, 0:tn], [yb.k], [], ych)
                S.barrier()

            with contextlib.ExitStack() as ph:
                of = S.sb([128, NCH, 256], F32, ph)
                qs = Rot(2, [128, 512], F32, ph)
                ks = Rot(2, [128, 512], F32, ph)
                zs = Rot(2, [16, 512], BF16, ph)
                vs = Rot(2, [128, 4, 256], BF16, ph)
                srs = Rot(2, [128, 4, 256], F32, ph)
                gtmp = [[S.sb([128, 512], F32, ph) for _ in range(8)] for _ in range(2)]
                btmp = [[S.sb([128, 512], BF16, ph) for _ in range(5)] for _ in range(2)]
                St = S.sb([128, 256], F32, ph)
                Sb = S.sb([128, 256], BF16, ph)
                PTs = [S.sb([128, 128], BF16, ph) for _ in range(2)]
                osum = [S.sb([128, 256], F32, ph) for _ in range(2)]
                gsb = [S.sb([128, 256], F32, ph) for _ in range(2)]
                ybuf = [S.sb([128, 256], F32, ph) for _ in range(2)]
                ss2 = S.sb([128, 2], F32, ph)
                y2s = Rot(2, [128, 2, 512], BF16, ph, 'sw')
                it = 0
                ci = 0
                for h in range(4):
                    for d in range(2):
                        order = list(range(len(tiles))) if d == 0 else [0] + list(range(len(tiles) - 1, 0, -1))
                        op('pool', lambda e, t_=St.t[:]: e.memset(t_, 0.0), [], [St.k])
                        op('pool', lambda e, t_=Sb.t[:]: e.memset(t_, 0.0), [], [Sb.k])
                        il, im = (127, 63) if d == 0 else (0, 64)
                        for ti in order:
                            t0, tn = tiles[ti]
                            nsub = tn // 128
                            G_, B_, D1, D2, E1, E2, E3, E4 = gtmp[it % 2]
                            kh, qt_, kt_, qb_, khT = btmp[it % 2]
                            it += 1
                            q_, qch = qs.next()
                            k_, kch = ks.next()
                            z_, zch = zs.next()
                            v_, vch = vs.next()
                            S.dma('sp', q_.t[:, 0:tn], gqT[h * 128:(h + 1) * 128, t0:t0 + tn], [], [q_.k], qch)
                            S.dma('sp', k_.t[:, 0:tn], gkT[h * 128:(h + 1) * 128, t0:t0 + tn], [], [k_.k], kch)
                            S.dma('sp', z_.t[:, 0:tn], zT[d, :, t0:t0 + tn], [], [z_.k], zch)
                            S.dma('sp', v_.t[:, 0:nsub, :], gvD[t0:t0 + tn, h * 256:(h + 1) * 256].rearrange("(s p) e -> p s e", p=128),
                                  [], [v_.k], vch)
                            if d == 1 and not (last and ti == 0):
                                sr_, sch = srs.next()
                                S.dma('sp', sr_.t[:, 0:nsub, :], srD[t0:t0 + tn, h * 256:(h + 1) * 256].rearrange("(s p) e -> p s e", p=128),
                                      [], [sr_.k], sch)
                            pg = PS[0]
                            mm(pg.t[:, 0:tn], gw.t[0:16, d, h * 128:(h + 1) * 128], z_.t[0:16, 0:tn], True, True, [gw.k, z_.k], [pg.k])
                            act(G_.t[:, 0:tn], pg.t[:, 0:tn], AF.Exp, [pg.k, negb.k], [G_.k], bias=negb.t[:, d * 4 + h:d * 4 + h + 1], scale=-1.0)
                            act(G_.t[:, 0:tn], G_.t[:, 0:tn], AF.Ln, [G_.k, one_t.k], [G_.k], bias=one_t.t[:])
                            if d == 0:
                                scan(B_.t[:, 0:tn], rm.t[:, 0, 0:tn], G_.t[:, 0:tn], 0.0, [rm.k, G_.k], [B_.k])
                            else:
                                scan(B_.t[:, 0:tn][:, ::-1], rm.t[:, 1, 0:tn][:, ::-1], G_.t[:, 0:tn][:, ::-1], 0.0, [rm.k, G_.k], [B_.k])
                            v3 = lambda b_: b_.t[:, 0:tn].rearrange("p (c k) -> p c k", k=128)
                            bl = v3(B_)[:, :, il:il + 1].broadcast_to([128, nsub, 128])
                            bm = v3(B_)[:, :, im:im + 1].broadcast_to([128, nsub, 128])
                            tt('dve', v3(D1), bl, v3(B_), ALU.subtract, [B_.k], [D1.k])
                            tt('dve', v3(D2), bm, v3(B_), ALU.subtract, [B_.k], [D2.k])
                            act(E1.t[:, 0:tn], D1.t[:, 0:tn], AF.Exp, [D1.k], [E1.k], scale=-1.0 / 16)
                            act(E2.t[:, 0:tn], D2.t[:, 0:tn], AF.Exp, [D2.k], [E2.k], scale=1.0 / 16)
                            act(E3.t[:, 0:tn], D2.t[:, 0:tn], AF.Exp, [D2.k], [E3.k], scale=-1.0 / 16)
                            act(E4.t[:, 0:tn], B_.t[:, 0:tn], AF.Exp, [B_.k], [E4.k], scale=-1.0 / 16)
                            tt('pool', kh.t[:, 0:tn], k_.t[:, 0:tn], E1.t[:, 0:tn], ALU.mult, [k_.k, E1.k], [kh.k])
                            tt('dve', qt_.t[:, 0:tn], q_.t[:, 0:tn], E2.t[:, 0:tn], ALU.mult, [q_.k, E2.k], [qt_.k])
                            tt('pool', kt_.t[:, 0:tn], k_.t[:, 0:tn], E3.t[:, 0:tn], ALU.mult, [k_.k, E3.k], [kt_.k])
                            tt('dve', qb_.t[:, 0:tn], q_.t[:, 0:tn], E4.t[:, 0:tn], ALU.mult, [q_.k, E4.k], [qb_.k])
                            for s in range(nsub):
                                tr(PB.t[:, s * 128:(s + 1) * 128], kh.t[:, s * 128:(s + 1) * 128], identb, [kh.k, cstb.k], [PB.k])
                            cp('act', khT.t[:, 0:tn], PB.t[:, 0:tn], [PB.k], [khT.k])
                            wr_y2 = d == 1 and not (last and ti == 0)
                            if wr_y2:
                                y2, y2ch = y2s.next()
                            corder = range(nsub) if d == 0 else range(nsub - 1, -1, -1)
                            for c in corder:
                                cs128 = slice(c * 128, (c + 1) * 128)
                                gc = t0 // 128 + c
                                psc, po, pu = PS[1 + (ci % 2) * 3], PS[2 + (ci % 2) * 3], PS[3 + (ci % 2) * 3]
                                PT = PTs[ci % 2]
                                ci += 1
                                mm(psc.t[:, 0:128], kt_.t[:, cs128], qt_.t[:, cs128], True, True, [kt_.k, qt_.k], [psc.k])
                                tt('dve', PT.t[:], psc.t[:, 0:128], cst.t[:, 3 + d, :], ALU.mult, [psc.k, cst.k], [PT.k])
                                mm(po.t[:, 0:256], qb_.t[:, cs128], Sb.t[:], True, False, [qb_.k, Sb.k], [po.k])
                                mm(po.t[:, 0:256], PT.t[:], v_.t[:, c, :], False, True, [PT.k, v_.k], [po.k])
                                mm(pu.t[:, 0:256], khT.t[:, cs128], v_.t[:, c, :], True, True, [khT.k, v_.k], [pu.k])
                                dcol = E4.t[:, c * 128 + il:c * 128 + il + 1]
                                stt(St.t[:], St.t[:], dcol, pu.t[:, 0:256], ALU.mult, ALU.add, [St.k, E4.k, pu.k], [St.k])
                                cp('act', Sb.t[:], St.t[:], [St.k], [Sb.k])
                                if d == 0:
                                    cp('act', of.t[:, gc, :], po.t[:, 0:256], [po.k], [of.k])
                                elif wr_y2:
                                    os_, gs_, yb_ = osum[ci % 2], gsb[ci % 2], ybuf[ci % 2]
                                    tt('dve', os_.t[:], po.t[:, 0:256], of.t[:, gc, :], ALU.add, [po.k, of.k], [os_.k])
                                    act(yb_.t[:], os_.t[:], AF.Square, [os_.k], [yb_.k, ss2.k], accum=ss2.t[:, 0:1])
                                    act(ss2.t[:, 1:2], ss2.t[:, 0:1], AF.Sqrt, [ss2.k, eps_t.k], [ss2.k], bias=eps_t.t[:], scale=1.0 / 256)
                                    op('dve', lambda e, t_=ss2.t[:, 1:2]: e.reciprocal(out=t_, in_=t_), [ss2.k], [ss2.k])
                                    tt('pool', gs_.t[:], sr_.t[:, c, :], pbc_t.t[:, 2048:2304], ALU.mult, [sr_.k, pbc_t.k], [gs_.k])
                                    stt(yb_.t[:], os_.t[:], ss2.t[:, 1:2], gs_.t[:], ALU.mult, ALU.mult, [os_.k, ss2.k, gs_.k], [yb_.k])
                                    pt2 = PS[0]
                                    for hh in range(2):
                                        tr(pt2.t[:, hh * 128:(hh + 1) * 128], yb_.t[:, hh * 128:(hh + 1) * 128], ident, [yb_.k, cst.k], [pt2.k])
                                    cp('act', y2.t[:, :, cs128], pt2.t[:, 0:256].rearrange("p (a b) -> p a b", a=2), [pt2.k], [y2.k])
                            if wr_y2:
                                S.dma('pool', y2T[h * 256:(h + 1) * 256, t0:t0 + tn].rearrange("(a p) t -> p a t", p=128),
                                      y2.t[:, :, 0:tn], [y2.k], [], y2ch)
                S.barrier()

            with contextlib.ExitStack() as ph:
                kTs = S.sb([128, S_], BF16, ph)
                Vs = S.sb([128, NCH, 128], BF16, ph)
                qts = Rot(2, [128, 512], BF16, ph)
                pTs = [S.sb([128, 512], BF16, ph) for _ in range(3)]
                rinv = S.sb([128, 512], F32, ph)
                y3s = Rot(2, [128, 512], BF16, ph, 'sw')
                ech = S.newchan()
                ech2 = S.newchan()
                pi = 0
                for g in range(2):
                    S.dma('sp', kTs.t[:], kT[g * 128:(g + 1) * 128, :], [], [kTs.k], ech)
                    for n0 in range(0, NCH, 8):
                        n1 = min(NCH, n0 + 8)
                        S.dma('sp', Vs.t[:, n0:n1, :], avD[n0 * 128:n1 * 128, g * 128:(g + 1) * 128].rearrange("(n p) e -> p n e", p=128),
                              [], [Vs.k], ech2)
                    for qh in range(4):
                        hh = g * 4 + qh
                        for ti, (t0, tn) in enumerate(tiles):
                            if last and ti == 0:
                                continue
                            nkb = LC // 128 if ti == 0 else NCH
                            qt_, qch = qts.next()
                            S.dma('sp', qt_.t[:, 0:tn], qT[hh * 128:(hh + 1) * 128, t0:t0 + tn], [], [qt_.k], qch)
                            po, psm = PS[0], PS[1]
                            pss = [PS[2], PS[3], PS[4]]

                            def score(kb):
                                p_ = pss[kb % 3]
                                mm(p_.t[:, 0:tn], kTs.t[:, kb * 128:(kb + 1) * 128], qt_.t[:, 0:tn], True, True, [kTs.k, qt_.k], [p_.k])
                            score(0)
                            for kb in range(nkb):
                                if kb + 1 < nkb:
                                    score(kb + 1)
                                p_ = pss[kb % 3]
                                pT = pTs[pi % 3]
                                pi += 1
                                act(pT.t[:, 0:tn], p_.t[:, 0:tn], AF.Exp, [p_.k], [pT.k])
                                mm(po.t[:, 0:tn], Vs.t[:, kb, :], pT.t[:, 0:tn], kb == 0, kb == nkb - 1, [Vs.k, pT.k], [po.k])
                                mm(psm.t[:, 0:tn], onesb, pT.t[:, 0:tn], kb == 0, kb == nkb - 1, [cstb.k, pT.k], [psm.k])
                            op('dve', lambda e, o_=rinv.t[:, 0:tn], i_=psm.t[:, 0:tn]: e.reciprocal(out=o_, in_=i_), [psm.k], [rinv.k])
                            y3, ych = y3s.next()
                            tt('dve', y3.t[:, 0:tn], po.t[:, 0:tn], rinv.t[:, 0:tn], ALU.mult, [po.k, rinv.k], [y3.k])
                            S.dma('pool', y3T[hh * 128:(hh + 1) * 128, t0:t0 + tn], y3.t[:, 0:tn], [y3.k], [], ych)
                S.barrier()

            with contextlib.ExitStack() as ph:
                wb3 = S.sb([128, 3, 8, D], BF16, ph)
                wo = S.sb([128, 8, D], BF16, ph)
                fch = S.newchan()
                for i in range(3):
                    S.dma('sp', wb3.t[:, i, :, :], w_br16[l, i].rearrange("(c p) n -> p c n", p=128), [('W16', l)], [wb3.k], fch)
                S.dma('sp', wo.t[:], w_out16[l].rearrange("(c p) n -> p c n", p=128), [('W16', l)], [wo.k], S.newchan())
                xts = Rot(2, [128, 4, D], F32, ph, 'both')
                yTs = [Rot(1, [128, 8, 512], BF16, ph) for _ in range(3)]
                g3s = Rot(2, [128, 3, 512], F32, ph)
                mT = S.sb([128, 8, 512], BF16, ph)
                mtmp = [[S.sb([128, 512], F32, ph) for _ in range(2)] for _ in range(2)]
                otmp = [S.sb([128, 512], F32, ph) for _ in range(2)]
                ysrc = (y1T, y2T, y3T)
                pbi = 0
                oi = 0
                for ti, (t0, tn) in enumerate(tiles):
                    if last and ti == 0:
                        continue
                    nsub = tn // 128
                    j = 1 if ti == 0 else 0
                    xt, xch = xts.next()
                    S.dma('sp', xt.t[:, 0:nsub, :], xsrc[t0:t0 + tn, :].rearrange("(s p) d -> p s d", p=128), [xt.k], [xt.k], xch)
                    ys = []
                    for i in range(3):
                        y_, ych = yTs[i].next()
                        S.dma('sp', y_.t[:, :, 0:tn], ysrc[i][:, t0:t0 + tn].rearrange("(c p) t -> p c t", p=128), [], [y_.k], ych)
                        ys.append(y_)
                    for dc in range(8):
                        g3, gch = g3s.next()
                        S.dma('sp', g3.t[:, :, 0:tn], bgT[:, t0:t0 + tn].rearrange("(i c p) t -> c p i t", i=3, p=128)[dc],
                              [], [g3.k], gch)
                        m0, m1 = mtmp[dc % 2]
                        for i in range(3):
                            pb = PS[pbi % 4]
                            pbi += 1
                            for kc in range(8):
                                mm(pb.t[:, 0:tn], wb3.t[:, i, kc, dc * 128:(dc + 1) * 128], ys[i].t[:, kc, 0:tn], kc == 0, kc == 7,
                                   [wb3.k, ys[i].k], [pb.k])
                            if i == 0:
                                tt('dve', m0.t[:, 0:tn], pb.t[:, 0:tn], g3.t[:, 0, 0:tn], ALU.mult, [pb.k, g3.k], [m0.k])
                            elif i == 1:
                                tt('dve', m1.t[:, 0:tn], pb.t[:, 0:tn], g3.t[:, 1, 0:tn], ALU.mult, [pb.k, g3.k], [m1.k])
                                tt('pool', m0.t[:, 0:tn], m0.t[:, 0:tn], m1.t[:, 0:tn], ALU.add, [m0.k, m1.k], [m0.k])
                            else:
                                tt('dve', m1.t[:, 0:tn], pb.t[:, 0:tn], g3.t[:, 2, 0:tn], ALU.mult, [pb.k, g3.k], [m1.k])
                                tt('pool', mT.t[:, dc, 0:tn], m0.t[:, 0:tn], m1.t[:, 0:tn], ALU.add, [m0.k, m1.k], [mT.k])
                    for s in range(nsub):
                        for cb in range(2):
                            pb = PS[4 + pbi % 3]
                            pbi += 1
                            for kc in range(8):
                                mm(pb.t[:], mT.t[:, kc, s * 128:(s + 1) * 128], wo.t[:, kc, cb * 512:(cb + 1) * 512], kc == 0, kc == 7,
                                   [mT.k, wo.k], [pb.k])
                            ot = otmp[oi % 2]
                            oi += 1
                            tt('dve', ot.t[:], pb.t[:], gaTM.t[:, j, 0, cb * 512:(cb + 1) * 512], ALU.mult, [pb.k, gaTM.k], [ot.k])
                            tt('pool', xt.t[:, s, cb * 512:(cb + 1) * 512], ot.t[:], xt.t[:, s, cb * 512:(cb + 1) * 512], ALU.add,
                               [ot.k, xt.k], [xt.k])
                    S.dma('pool', xres[t0:t0 + tn, :].rearrange("(s p) d -> p s d", p=128), xt.t[:, 0:nsub, :], [xt.k], [], xts.store_chan())
                S.barrier()

            with contextlib.ExitStack() as ph:
                xts = Rot(2, [128, 4, D], F32, ph, 'both')
                xn = S.sb([128, 4, D], F32, ph)
                hT = S.sb([128, 8, 512], BF16, ph)
                ssq = S.sb([128, 4], F32, ph)
                rstd = S.sb([128, 4], F32, ph)
                junk = S.sb([128, D], F32, ph)
                wbl = Rot(6, [128, 8, 512], BF16, ph)
                hid = S.sb([128, 32, 512], BF16, ph)
                rtmp = [S.sb([128, 512], F32, ph) for _ in range(2)]
                otmp = [S.sb([128, 512], F32, ph) for _ in range(2)]
                pbi = 0
                ri = 0
                for ti, (t0, tn) in enumerate(tiles):
                    if last and ti == 0:
                        continue
                    nsub = tn // 128
                    j = 1 if ti == 0 else 0
                    xt, xch = xts.next()
                    S.dma('sp', xt.t[:, 0:nsub, :], xres[t0:t0 + tn, :].rearrange("(s p) d -> p s d", p=128), [xt.k], [xt.k], xch)
                    norm_front(ph, xt, nsub, j, 1, hT, xn, ssq, rstd, junk, [PS[0], PS[1]])
                    for blk in range(8):
                        wb_, wch = wbl.next()
                        S.dma('sp', wb_.t[:], w116[l, :, blk * 512:(blk + 1) * 512].rearrange("(c p) n -> p c n", p=128),
                              [('W16', l)], [wb_.k], wch)
                        for g in range(4):
                            pb = PS[2 + pbi % 3]
                            pbi += 1
                            for kc in range(8):
                                mm(pb.t[:, 0:tn], wb_.t[:, kc, g * 128:(g + 1) * 128], hT.t[:, kc, 0:tn], kc == 0, kc == 7,
                                   [wb_.k, hT.k], [pb.k])
                            rt = rtmp[ri % 2]
                            ri += 1
                            act(rt.t[:, 0:tn], pb.t[:, 0:tn], AF.Relu, [pb.k], [rt.k])
                            tt('pool', hid.t[:, blk * 4 + g, 0:tn], rt.t[:, 0:tn], rt.t[:, 0:tn], ALU.mult, [rt.k], [hid.k])
                    for cb in range(2):
                        wbs = []
                        for rg in range(4):
                            wb_, wch = wbl.next()
                            S.dma('sp', wb_.t[:], w216[l, rg * 1024:(rg + 1) * 1024, cb * 512:(cb + 1) * 512].rearrange("(c p) n -> p c n", p=128),
                                  [('W16', l)], [wb_.k], wch)
                            wbs.append(wb_)
                        for s in range(nsub):
                            pb = PS[5 + pbi % 2]
                            pbi += 1
                            for kc in range(32):
                                mm(pb.t[:], hid.t[:, kc, s * 128:(s + 1) * 128], wbs[kc // 8].t[:, kc % 8, :], kc == 0, kc == 31,
                                   [hid.k, wbs[kc // 8].k], [pb.k])
                            ot = otmp[ri % 2]
                            ri += 1
                            tt('dve', ot.t[:], pb.t[:], gaTM.t[:, j, 1, cb * 512:(cb + 1) * 512], ALU.mult, [pb.k, gaTM.k], [ot.k])
                            tt('pool', xt.t[:, s, cb * 512:(cb + 1) * 512], ot.t[:], xt.t[:, s, cb * 512:(cb + 1) * 512], ALU.add,
                               [ot.k, xt.k], [xt.k])
                    if last:
                        dst = out[t0 - LC:t0 - LC + tn, :]
                    else:
                        dst = xres[t0:t0 + tn, :]
                    S.dma('pool', dst.rearrange("(s p) d -> p s d", p=128), xt.t[:, 0:nsub, :], [xt.k], [], xts.store_chan())
                S.barrier()
        S.emit()
    return nc


def _fm(v):
    return np.ascontiguousarray(np.asarray(v, np.float32).reshape(-1, 128).T)


def host_consts(N):
    k = np.arange(128)
    ident = np.eye(128, dtype=np.float32)
    ones_s = np.full((128, 128), 1.0 / 128, np.float32)
    Pm = np.zeros((128, 128), np.float32)
    for base in (0, 64):
        for i in range(32):
            Pm[base + i, base + 32 + i] = -1.0
            Pm[base + 32 + i, base + i] = 1.0
    rotT = np.ascontiguousarray(Pm.T)
    m0 = (k[:, None] <= k[None, :]).astype(np.float32)
    m1 = (k[:, None] >= k[None, :]).astype(np.float32)
    ones = np.ones((128, 128), np.float32)
    consts = np.ascontiguousarray(np.stack([ident, ones_s, rotT, m0, m1, ones], axis=1))
    t = np.arange(512)
    rmask = np.ones((128, 2, 512), np.float32)
    rmask[:, 0, t % 128 == 0] = 0.0
    rmask[:, 1, t % 128 == 127] = 0.0
    rows = N // 64
    row_pos = np.repeat(np.arange(rows), 64).astype(np.float32)
    col_pos = np.tile(np.arange(64), rows).astype(np.float32)
    inv_freq = (np.float32(10000.0) ** (-np.arange(0, 64, 2, dtype=np.float32) / np.float32(64))).astype(np.float32)
    ang_r = (row_pos[:, None] * inv_freq[None, :]).astype(np.float32)
    ang_c = (col_pos[:, None] * inv_freq[None, :]).astype(np.float32)
    cosT = np.concatenate([np.cos(ang_r).T, np.cos(ang_r).T, np.cos(ang_c).T, np.cos(ang_c).T], axis=0)
    sinT = np.concatenate([np.sin(ang_r).T, np.sin(ang_r).T, np.sin(ang_c).T, np.sin(ang_c).T], axis=0)
    return consts, rmask, np.ascontiguousarray(cosT.astype(np.float32)), np.ascontiguousarray(sinT.astype(np.float32))


def host_layout(inp, DEPTH):
    pvec = np.zeros((DEPTH, 128, PV_W), np.float32)
    pbc = np.zeros((DEPTH, 128, PBC_W), np.float32)
    for l in range(DEPTH):
        pvec[l, :, PV_N1G:PV_N1G + 8] = _fm(inp['norm1_g'][l])
        pvec[l, :, PV_N2G:PV_N2G + 8] = _fm(inp['norm2_g'][l])
        mb = np.asarray(inp['mod_b'][l], np.float32)
        for vi, c0 in enumerate((0, 1024, 3072, 4096)):
            pvec[l, :, PV_MODB + vi * 8:PV_MODB + vi * 8 + 8] = _fm(mb[c0:c0 + 1024])
        cw = np.asarray(inp['conv_w'][l], np.float32)
        pvec[l, :, PV_CW:PV_CW + 32] = cw.reshape(4, 8, 128).transpose(2, 1, 0).reshape(128, 32)
        pvec[l, :, PV_CB:PV_CB + 8] = _fm(inp['conv_b'][l])
        for d in range(2):
            pvec[l, :, PV_AB + d * 8:PV_AB + d * 8 + 8] = _fm(inp['lru_a_b'][l, d])
            pvec[l, :, PV_XB + d * 8:PV_XB + d * 8 + 8] = _fm(inp['lru_x_b'][l, d])
            pvec[l, :, PV_LAM + d * 8:PV_LAM + d * 8 + 8] = _fm(inp['lru_lambda'][l, d])
            pvec[l, :, PV_GB + d * 4:PV_GB + d * 4 + 4] = _fm(inp['gla_gate_b'][l, d])
        pvec[l, :, PV_QG] = np.asarray(inp['q_norm_g'][l], np.float32)
        pvec[l, :, PV_KG] = np.asarray(inp['k_norm_g'][l], np.float32)
        pbc[l, :, 0:1024] = mb[None, 2048:3072]
        pbc[l, :, 1024:2048] = mb[None, 5120:6144]
        pbc[l, :, 2048:2304] = np.asarray(inp['gla_norm_g'][l], np.float32)[None, :]
    return pvec, pbc


_CACHE = {}


def run(inputs, N, LC, DEPTH, ncores, debug=False):
    key = (N, LC, DEPTH, debug)
    if key not in _CACHE:
        _CACHE[key] = build(N, LC, DEPTH, debug)
    nc = _CACHE[key]
    f = lambda a: np.ascontiguousarray(np.asarray(a, np.float32))
    consts, rmask, cosT, sinT = host_consts(N)
    pvec, pbc = host_layout(inputs, DEPTH)
    shared = dict(pvec=pvec, pbc=pbc, consts=consts, rmask=rmask, cosT=cosT, sinT=sinT)
    for k in ('mod_w', 'w_in', 'lru_a_w', 'lru_x_w', 'gla_gate_w', 'w_branch', 'w_out', 'mlp_w1', 'mlp_w2'):
        shared[k] = f(inputs[k])[:DEPTH]
    in_maps = []
    for b in range(ncores):
        m = dict(shared)
        m['xin'] = np.ascontiguousarray(np.concatenate([f(inputs['ctx'][b]), f(inputs['x'][b])], axis=0))
        ccT = np.stack([_fm(inputs['c'][b]), _fm(inputs['c_ctx'])], axis=2)
        m['ccT'] = np.ascontiguousarray(ccT)
        in_maps.append(m)
    res = run_bass_kernel_spmd(nc, in_maps, core_ids=list(range(ncores)))
    return res


def kernel(**inputs):
    B, N, _ = inputs['x'].shape
    LC = inputs['ctx'].shape[1]
    DEPTH = inputs['w_in'].shape[0]
    res = run(inputs, N, LC, DEPTH, B)
    return np.stack([np.asarray(r['out'], np.float32) for r in res.results], axis=0)
```

```python
import contextlib
import numpy as np
import ml_dtypes
import concourse.bass as bass
import concourse.mybir as mybir
from concourse.bass_utils import run_bass_kernel_spmd

F32 = mybir.dt.float32
BF16 = mybir.dt.bfloat16
AF = mybir.ActivationFunctionType
ALU = mybir.AluOpType

D = 1024
DPROJ = 9760
EPS = 1e-6


class Chan:
    def __init__(self, sem):
        self.sem = sem
        self.count = 0


class Buf:
    def __init__(self, t, k):
        self.t = t
        self.k = k


class Sched:
    ENG = ['pe', 'act', 'dve', 'pool', 'sp']

    def __init__(self, nc, stack, nsem=96):
        self.nc = nc
        self.stack = stack
        self.free = [stack.enter_context(nc.semaphore("s%d" % i)) for i in range(nsem)]
        self.q = {e: [] for e in self.ENG}
        self.echan = {e: Chan(self.free.pop()) for e in ['pe', 'act', 'dve', 'pool']}
        self.chans = []
        self.chanpool = {'hw': [], 'sw': []}
        self.scoped = []
        self.seen = {e: {} for e in self.ENG}
        self.res = {}
        self.nt = 0

    def newchan(self, kind='hw', barrier=True, scoped=True):
        if barrier and scoped and self.chanpool[kind]:
            c = self.chanpool[kind].pop()
        else:
            c = Chan(self.free.pop())
            c.kind = kind
            if barrier:
                self.chans.append(c)
        if barrier and scoped:
            self.scoped.append(c)
        return c

    def sb(self, shape, dtype, stack=None):
        self.nt += 1
        st = stack or self.stack
        t = st.enter_context(self.nc.sbuf_tensor("t%d" % self.nt, list(shape), dtype))
        return Buf(t, "b%d" % self.nt)

    def ps(self, shape, dtype):
        self.nt += 1
        t = self.stack.enter_context(self.nc.psum_tensor("p%d" % self.nt, list(shape), dtype))
        return Buf(t, "p%d" % self.nt)

    def _deps(self, eng, reads, writes, skip=None):
        need = {}

        def add(d):
            for cid, (ch, v) in d.items():
                if skip is not None and ch is skip:
                    continue
                if self.seen[eng].get(cid, 0) >= v:
                    continue
                if need.get(cid, (None, 0))[1] < v:
                    need[cid] = (ch, v)
        for k in reads:
            r = self.res.get(k)
            if r:
                add(r['w'])
        for k in writes:
            r = self.res.get(k)
            if r:
                add(r['w'])
                add(r['r'])
        for cid, (ch, v) in need.items():
            self.seen[eng][cid] = v
            self.q[eng].append(('wait', ch.sem, v))

    def _commit(self, ev, reads, writes):
        ch, v = ev
        for k in reads:
            self.res.setdefault(k, {'w': {}, 'r': {}})['r'][id(ch)] = ev
        for k in writes:
            self.res[k] = {'w': {id(ch): ev}, 'r': {}}

    def op(self, eng, fn, reads=(), writes=()):
        skip = self.echan['pe'] if eng == 'pe' else None
        self._deps(eng, reads, writes, skip=skip)
        ch = self.echan[eng]
        ch.count += 1
        self.q[eng].append(('op', fn, ch.sem))
        self._commit((ch, ch.count), reads, writes)

    def dma(self, q, out, in_, reads, writes, chan):
        assert chan.kind == ('sw' if q == 'pool' else 'hw'), (q, chan.kind)
        self._deps(q, reads, writes)
        chan.count += 16
        self.q[q].append(('dma', out, in_, chan.sem))
        self._commit((chan, chan.count), reads, writes)

    def barrier(self):
        allc = list(self.echan.values()) + self.chans
        for e in self.ENG:
            for ch in allc:
                if ch.count > self.seen[e].get(id(ch), 0):
                    self.seen[e][id(ch)] = ch.count
                    self.q[e].append(('wait', ch.sem, ch.count))
        self.res = {k: v for k, v in self.res.items() if isinstance(k, tuple)}
        for c in self.scoped:
            self.chanpool[c.kind].append(c)
        self.scoped = []

    def _replay(self, e, name):
        for it in self.q[name]:
            if it[0] == 'wait':
                e.wait_ge(it[1], it[2])
            elif it[0] == 'op':
                it[1](e).then_inc(it[2], 1)
            else:
                e.dma_start(out=it[1], in_=it[2]).then_inc(it[3], 16)

    def emit(self):
        with self.nc.Block() as block:
            @block.sync
            def _(e):
                self._replay(e, 'sp')

            @block.tensor
            def _(e):
                self._replay(e, 'pe')

            @block.scalar
            def _(e):
                self._replay(e, 'act')

            @block.vector
            def _(e):
                self._replay(e, 'dve')

            @block.gpsimd
            def _(e):
                self._replay(e, 'pool')


PV_N1G, PV_N2G, PV_MODB, PV_CW, PV_CB, PV_AB, PV_XB, PV_LAM, PV_GB, PV_QG, PV_KG, PV_W = \
    0, 8, 16, 48, 80, 88, 104, 120, 136, 144, 145, 146
PBC_W = 2048 + 256

IN_BLOCKS = ([('lx', 0 + 512 * i, 512) for i in range(2)] + [('ly', 1024 + 512 * i, 512) for i in range(2)] +
             [('gq', 2048, 512), ('gk', 2560, 512)] + [('gv', 3072 + 512 * i, 512) for i in range(2)] +
             [('gr', 4096 + 512 * i, 512) for i in range(2)] + [('gz', 5120, 32)] +
             [('aq', 5152 + 512 * i, 512) for i in range(2)] + [('ak', 6176, 256), ('av', 6432, 256)] +
             [('bg', 6688 + 512 * i, 512) for i in range(6)])


def build(N, LC, DEPTH, debug=False):
    S_ = LC + N
    NT = N // 512
    tiles = [(0, LC)] + [(LC + 512 * i, 512) for i in range(NT)]
    NCH = S_ // 128
    nc = bass.Bass("TRN2", target_bir_lowering=False)

    def din(name, shape, dt=F32):
        return nc.dram_tensor(name, list(shape), dt, kind="ExternalInput").ap()

    def dscr(name, shape, dt):
        return nc.dram_tensor(name, list(shape), dt, kind="ExternalOutput" if debug else "Internal").ap()

    xin = din("xin", [S_, D])
    ccT = din("ccT", [128, 8, 2])
    pvec = din("pvec", [DEPTH, 128, PV_W])
    pbc = din("pbc", [DEPTH, 128, PBC_W])
    mod_w = din("mod_w", [DEPTH, D, 6 * D])
    w_in = din("w_in", [DEPTH, D, DPROJ])
    lru_a_w = din("lru_a_w", [DEPTH, 2, 16, 64, 64])
    lru_x_w = din("lru_x_w", [DEPTH, 2, 16, 64, 64])
    gla_gate_w = din("gla_gate_w", [DEPTH, 2, 16, 512])
    w_branch = din("w_branch", [DEPTH, 3, D, D])
    w_out = din("w_out", [DEPTH, D, D])
    mlp_w1 = din("mlp_w1", [DEPTH, D, 4 * D])
    mlp_w2 = din("mlp_w2", [DEPTH, 4 * D, D])
    cosT = din("cosT", [128, N])
    sinT = din("sinT", [128, N])
    consts = din("consts", [128, 6, 128])
    rmask = din("rmask", [128, 2, 512])
    out = nc.dram_tensor("out", [N, D], F32, kind="ExternalOutput").ap()

    xres = dscr("xres", [S_, D], F32)
    lxT = dscr("lxT", [D, S_], F32)
    gyT = dscr("gyT", [D, S_], F32)
    gqT = dscr("gqT", [512, S_], F32)
    gkT = dscr("gkT", [512, S_], F32)
    zT = dscr("zT", [2, 16, S_], BF16)
    gvD = dscr("gvD", [S_, D], BF16)
    srD = dscr("srD", [S_, D], F32)
    qT = dscr("qT", [D, S_], BF16)
    kT = dscr("kT", [256, S_], BF16)
    avD = dscr("avD", [S_, 256], BF16)
    bgT = dscr("bgT", [3 * D, S_], F32)
    y1T = dscr("y1T", [D, S_], BF16)
    y2T = dscr("y2T", [D, S_], BF16)
    y3T = dscr("y3T", [D, S_], BF16)
    w_in16 = dscr("w_in16", [DEPTH, D, DPROJ], BF16)
    w_br16 = dscr("w_br16", [DEPTH, 3, D, D], BF16)
    w_out16 = dscr("w_out16", [DEPTH, D, D], BF16)
    w116 = dscr("w116", [DEPTH, D, 4 * D], BF16)
    w216 = dscr("w216", [DEPTH, 4 * D, D], BF16)

    with contextlib.ExitStack() as st:
        S = Sched(nc, st)
        op = S.op

        def act(out_, in_, func, r, w, bias=None, scale=None, accum=None):
            kw = {}
            if bias is not None:
                kw['bias'] = bias
            if scale is not None:
                kw['scale'] = scale
            if accum is not None:
                kw['accum_out'] = accum
            op('act', lambda e: e.activation(out=out_, in_=in_, func=func, **kw), r, w)

        def tt(eng, out_, a, b, o, r, w):
            op(eng, lambda e: e.tensor_tensor(out=out_, in0=a, in1=b, op=o), r, w)

        def ts(eng, out_, a, s1, s2, o0, o1, r, w):
            if s2 is None:
                op(eng, lambda e: e.tensor_scalar(out=out_, in0=a, scalar1=s1, scalar2=None, op0=o0), r, w)
            else:
                op(eng, lambda e: e.tensor_scalar(out=out_, in0=a, scalar1=s1, scalar2=s2, op0=o0, op1=o1), r, w)

        def stt(out_, a, sc, b, o0, o1, r, w):
            op('dve', lambda e: e.scalar_tensor_tensor(out=out_, in0=a, scalar=sc, in1=b, op0=o0, op1=o1), r, w)

        def cp(eng, out_, in_, r, w):
            if eng == 'act':
                op('act', lambda e: e.copy(out=out_, in_=in_), r, w)
            else:
                op(eng, lambda e: e.tensor_copy(out=out_, in_=in_), r, w)

        def mm(out_, lhsT, rhs, start, stop, r, w):
            op('pe', lambda e: e.matmul(out_, lhsT=lhsT, rhs=rhs, start=start, stop=stop), r, w)

        def tr(out_, in_, ident, r, w):
            op('pe', lambda e: e.transpose(out_, in_, ident), r, w)

        def scan(out_, d0, d1, init, r, w):
            op('dve', lambda e: e.tensor_tensor_scan(out=out_, data0=d0, data1=d1, initial=init,
                                                     op0=ALU.mult, op1=ALU.add), r, w)

        class Rot:
            def __init__(self, n, shape, dt, stack, kind='hw'):
                self.b = [S.sb(shape, dt, stack) for _ in range(n)]
                self.c = [S.newchan('hw' if kind == 'both' else kind) for _ in range(n)]
                self.c2 = [S.newchan('sw') for _ in range(n)] if kind == 'both' else None
                self.i = 0

            def store_chan(self):
                return self.c2[(self.i - 1) % len(self.b)]

            def next(self):
                j = self.i % len(self.b)
                self.i += 1
                return self.b[j], self.c[j]

        PS = [S.ps([128, 512], F32) for _ in range(7)]
        PB = S.ps([128, 1024], BF16)
        cst = S.sb([128, 6, 128], F32)
        ident = cst.t[:, 0, :]
        ones_s = cst.t[:, 1, :]
        rotT = cst.t[:, 2, :]
        cstb = S.sb([128, 6, 128], BF16)
        identb = cstb.t[:, 0, :]
        onesb = cstb.t[:, 5, :]
        rm = S.sb([128, 2, 512], F32)
        pv = S.sb([128, DEPTH, PV_W], F32)
        eps_t = S.sb([128, 1], F32)
        one_t = S.sb([128, 1], F32)
        cs = S.sb([128, 8, 2], F32)
        csB = S.sb([128, 16, 128], F32)
        S.dma('sp', cst.t[:], consts, [], [cst.k], S.newchan(scoped=False))
        S.dma('sp', rm.t[:], rmask, [], [rm.k], S.newchan(scoped=False))
        S.dma('sp', pv.t[:], pvec.rearrange("l p w -> p l w"), [], [pv.k], S.newchan(scoped=False))
        S.dma('sp', cs.t[:], ccT, [], [cs.k], S.newchan(scoped=False))
        op('pool', lambda e: e.memset(eps_t.t[:], EPS), [], [eps_t.k])
        op('pool', lambda e: e.memset(one_t.t[:], 1.0), [], [one_t.k])
        cp('dve', cstb.t[:], cst.t[:], [cst.k], [cstb.k])
        act(cs.t[:], cs.t[:], AF.Silu, [cs.k], [cs.k])
        cp('dve', csB.t[:], cs.t[:].rearrange("p c j -> p (c j)").unsqueeze(2).broadcast_to([128, 16, 128]),
           [cs.k], [csB.k])

        wcs = [S.newchan('sw', barrier=False), S.newchan('sw', barrier=False)]

        def cast_layer(l):
            k = ('W16', l)
            wc = wcs[l % 2]
            for r0 in range(0, D, 256):
                S.dma('pool', w_in16[l, r0:r0 + 256, :], w_in[l, r0:r0 + 256, :], [], [], wc)
                S.dma('pool', w116[l, r0:r0 + 256, :], mlp_w1[l, r0:r0 + 256, :], [], [], wc)
                S.dma('pool', w_out16[l, r0:r0 + 256, :], w_out[l, r0:r0 + 256, :], [], [], wc)
                for i in range(3):
                    S.dma('pool', w_br16[l, i, r0:r0 + 256, :], w_branch[l, i, r0:r0 + 256, :], [], [], wc)
            for r0 in range(0, 4 * D, 512):
                S.dma('pool', w216[l, r0:r0 + 512, :], mlp_w2[l, r0:r0 + 512, :], [], [], wc)
            S.res[k] = {'w': {id(wc): (wc, wc.count)}, 'r': {}}

        cast_layer(0)

        modT = S.sb([128, 32, 2], F32)
        gm = S.sb([128, 2, 8, 2], F32)
        gaTM = S.sb([128, 2, 2, D], F32)
        pbc_t = S.sb([128, PBC_W], F32)
        lrc = S.sb([128, 2, 2, 8], F32)
        negb = S.sb([128, 8], F32)
        qkg = S.sb([128, 2], F32)
        bd = S.sb([128, 2, 2, 8, 128], BF16)
        gw = S.sb([16, 2, 512], BF16)
        op('pool', lambda e: e.memset(bd.t[:], 0.0), [], [bd.k])

        def norm_front(ph, xt, nsub, j, which, hT, xn, ssq, rstd, junk, pbank):
            for s in range(nsub):
                act(junk.t[:], xt.t[:, s, :], AF.Square, [xt.k], [junk.k, ssq.k], accum=ssq.t[:, s:s + 1])
            act(rstd.t[:, 0:nsub], ssq.t[:, 0:nsub], AF.Sqrt, [ssq.k, eps_t.k], [rstd.k], bias=eps_t.t[:], scale=1.0 / D)
            op('dve', lambda e, t_=rstd.t[:, 0:nsub]: e.reciprocal(out=t_, in_=t_), [rstd.k], [rstd.k])
            for s in range(nsub):
                ts('pool', xn.t[:, s, :], xt.t[:, s, :], rstd.t[:, s:s + 1], None, ALU.mult, None, [xt.k, rstd.k], [xn.k])
            for c in range(8):
                pb = pbank[c % len(pbank)]
                for s in range(nsub):
                    tr(pb.t[:, s * 128:(s + 1) * 128], xn.t[:, s, c * 128:(c + 1) * 128], ident, [xn.k, cst.k], [pb.k])
                act(hT.t[:, c, 0:nsub * 128], pb.t[:, 0:nsub * 128], AF.Identity, [pb.k, gm.k, modT.k], [hT.k],
                    bias=modT.t[:, (0 if which == 0 else 16) + c, j:j + 1], scale=gm.t[:, which, c, j:j + 1])

        for l in range(DEPTH):
            xsrc = xin if l == 0 else xres
            last = (l == DEPTH - 1)
            with contextlib.ExitStack() as ph:
                mwb = Rot(2, [128, 8, 512], F32, ph)
                S.dma('sp', pbc_t.t[:], pbc[l], [], [pbc_t.k], S.newchan())
                S.dma('pool', gw.t[:], gla_gate_w[l].rearrange("d r n -> r d n"), [], [gw.k], S.newchan('sw'))
                bdc = S.newchan('sw')
                for d in range(2):
                    for gi, wsrc in enumerate((lru_a_w, lru_x_w)):
                        for half in range(2):
                            src = wsrc[l, d].rearrange("(c two) i j -> two i c j", two=2)[half]
                            S.dma('pool', bd.t[half * 64:(half + 1) * 64, d, gi, :, half * 64:(half + 1) * 64], src,
                                  [], [bd.k], bdc)
                pm = PS[0]
                fm_cols = [0, 1024, 3072, 4096]
                for vi, c0 in enumerate(fm_cols):
                    for hb in range(2):
                        wb_, wch = mwb.next()
                        S.dma('sp', wb_.t[:], mod_w[l, :, c0 + hb * 512:c0 + (hb + 1) * 512].rearrange("(c p) n -> p c n", p=128),
                              [], [wb_.k], wch)
                        for g in range(4):
                            idx = vi * 8 + hb * 4 + g
                            for kc in range(8):
                                mm(pm.t[:, 2 * idx:2 * idx + 2], wb_.t[:, kc, g * 128:(g + 1) * 128], cs.t[:, kc, :],
                                   kc == 0, kc == 7, [wb_.k, cs.k], [pm.k])
                for j in range(2):
                    tt('dve', modT.t[:, :, j], pm.t[:, 0:64].rearrange("p (i j) -> p i j", j=2)[:, :, j],
                       pv.t[:, l, PV_MODB:PV_MODB + 32], ALU.add, [pm.k, pv.k], [modT.k])
                for which, (sc0, g0) in enumerate(((8, PV_N1G), (24, PV_N2G))):
                    for j in range(2):
                        stt(gm.t[:, which, :, j], modT.t[:, sc0:sc0 + 8, j], 1.0, pv.t[:, l, g0:g0 + 8], ALU.add, ALU.mult,
                            [modT.k, pv.k], [gm.k])
                for which, c0 in enumerate((2048, 5120)):
                    for hb in range(2):
                        wb_, wch = mwb.next()
                        S.dma('sp', wb_.t[:], mod_w[l, :, c0 + hb * 512:c0 + (hb + 1) * 512].rearrange("(c p) n -> p c n", p=128),
                              [], [wb_.k], wch)
                        for j in range(2):
                            pg = PS[1 + j]
                            for kc in range(8):
                                mm(pg.t[:], csB.t[:, kc * 2 + j, :], wb_.t[:, kc, :], kc == 0, kc == 7, [wb_.k, csB.k], [pg.k])
                            tt('dve', gaTM.t[:, j, which, hb * 512:(hb + 1) * 512], pg.t[:],
                               pbc_t.t[:, which * 1024 + hb * 512:which * 1024 + (hb + 1) * 512], ALU.add,
                               [pg.k, pbc_t.k], [gaTM.k])
                lam = pv.t[:, l, PV_LAM:PV_LAM + 16].rearrange("p (d c) -> p d c", d=2)
                act(lrc.t[:, :, 0, :], lam, AF.Exp, [pv.k], [lrc.k], scale=-1.0)
                act(lrc.t[:, :, 0, :], lrc.t[:, :, 0, :], AF.Ln, [lrc.k, one_t.k], [lrc.k], bias=one_t.t[:])
                ts('dve', lrc.t[:, :, 1, :], lrc.t[:, :, 0, :], -16.0, None, ALU.mult, None, [lrc.k], [lrc.k])
                ts('dve', lrc.t[:, :, 0, :], lrc.t[:, :, 0, :], -8.0, None, ALU.mult, None, [lrc.k], [lrc.k])
                ts('dve', negb.t[:], pv.t[:, l, PV_GB:PV_GB + 8], -1.0, None, ALU.mult, None, [pv.k], [negb.k])
                ts('dve', qkg.t[:, 0:1], pv.t[:, l, PV_QG:PV_QG + 1], 128.0 ** -0.5, None, ALU.mult, None, [pv.k], [qkg.k])
                cp('dve', qkg.t[:, 1:2], pv.t[:, l, PV_KG:PV_KG + 1], [pv.k], [qkg.k])
                S.barrier()

            with contextlib.ExitStack() as ph:
                xts = Rot(2, [128, 4, D], F32, ph)
                xn = S.sb([128, 4, D], F32, ph)
                hTs = [S.sb([128, 8, 512], BF16, ph) for _ in range(2)]
                ssq = S.sb([128, 4], F32, ph)
                rstd = S.sb([128, 4], F32, ph)
                junk = S.sb([128, D], F32, ph)
                wbl = Rot(4, [128, 8, 512], BF16, ph)
                stf = Rot(4, [128, 512], F32, ph, 'sw')
                stb = Rot(4, [128, 512], BF16, ph, 'sw')
                cst_ = Rot(2, [128, 2, 512], F32, ph)
                qtmp = [[S.sb([128, 512], F32, ph) for _ in range(4)] for _ in range(2)]
                qi = 0
                pbi = 0
                for ti, (t0, tn) in enumerate(tiles):
                    nsub = tn // 128
                    j = 1 if ti == 0 else 0
                    xt, xch = xts.next()
                    S.dma('sp', xt.t[:, 0:nsub, :], xsrc[t0:t0 + tn, :].rearrange("(s p) d -> p s d", p=128), [], [xt.k], xch)
                    if ti > 0:
                        cs_, cch = cst_.next()
                        S.dma('sp', cs_.t[:, 0, :], cosT[:, t0 - LC:t0 - LC + tn], [], [cs_.k], cch)
                        S.dma('sp', cs_.t[:, 1, :], sinT[:, t0 - LC:t0 - LC + tn], [], [cs_.k], cch)
                    hT = hTs[ti % 2]
                    norm_front(ph, xt, nsub, j, 0, hT, xn, ssq, rstd, junk, [PS[0], PS[1]])
                    for (kind, c0, ncols) in IN_BLOCKS:
                        wb_, wch = wbl.next()
                        S.dma('sp', wb_.t[:, :, 0:ncols], w_in16[l, :, c0:c0 + ncols].rearrange("(c p) n -> p c n", p=128),
                              [('W16', l)], [wb_.k], wch)
                        if kind in ('gv', 'gr', 'av'):
                            for s in range(nsub):
                                pb = PS[2 + pbi % 5]
                                pbi += 1
                                for kc in range(8):
                                    mm(pb.t[:, 0:ncols], hT.t[:, kc, s * 128:(s + 1) * 128], wb_.t[:, kc, 0:ncols],
                                       kc == 0, kc == 7, [hT.k, wb_.k], [pb.k])
                                r0 = t0 + s * 128
                                if kind == 'gv':
                                    sg, sc_ = stb.next()
                                    cp('dve', sg.t[:, 0:ncols], pb.t[:, 0:ncols], [pb.k], [sg.k])
                                    S.dma('pool', gvD[r0:r0 + 128, c0 - 3072:c0 - 3072 + ncols], sg.t[:, 0:ncols], [sg.k], [], sc_)
                                elif kind == 'gr':
                                    sg, sc_ = stf.next()
                                    act(sg.t[:, 0:ncols], pb.t[:, 0:ncols], AF.Silu, [pb.k], [sg.k])
                                    S.dma('pool', srD[r0:r0 + 128, c0 - 4096:c0 - 4096 + ncols], sg.t[:, 0:ncols], [sg.k], [], sc_)
                                else:
                                    sg, sc_ = stb.next()
                                    cp('dve', sg.t[:, 0:ncols], pb.t[:, 0:ncols], [pb.k], [sg.k])
                                    S.dma('pool', avD[r0:r0 + 128, :], sg.t[:, 0:ncols], [sg.k], [], sc_)
                            continue
                        gwid = 16 if kind == 'gz' else 128
                        for g in range(ncols // gwid):
                            pb = PS[2 + pbi % 5]
                            pbi += 1
                            for kc in range(8):
                                mm(pb.t[0:gwid, 0:tn], wb_.t[:, kc, g * gwid:(g + 1) * gwid], hT.t[:, kc, 0:tn],
                                   kc == 0, kc == 7, [hT.k, wb_.k], [pb.k])
                            f0 = c0 + g * gwid
                            if kind == 'lx':
                                sg, sc_ = stf.next()
                                cp('dve', sg.t[:, 0:tn], pb.t[:, 0:tn], [pb.k], [sg.k])
                                S.dma('pool', lxT[f0:f0 + 128, t0:t0 + tn], sg.t[:, 0:tn], [sg.k], [], sc_)
                            elif kind == 'ly':
                                sg, sc_ = stf.next()
                                act(sg.t[:, 0:tn], pb.t[:, 0:tn], AF.Gelu_apprx_tanh, [pb.k], [sg.k])
                                S.dma('pool', gyT[f0 - 1024:f0 - 1024 + 128, t0:t0 + tn], sg.t[:, 0:tn], [sg.k], [], sc_)
                            elif kind == 'gq':
                                sg, sc_ = stf.next()
                                ts('dve', sg.t[:, 0:tn], pb.t[:, 0:tn], 128.0 ** -0.5, None, ALU.mult, None, [pb.k], [sg.k])
                                S.dma('pool', gqT[f0 - 2048:f0 - 2048 + 128, t0:t0 + tn], sg.t[:, 0:tn], [sg.k], [], sc_)
                            elif kind == 'gk':
                                sg, sc_ = stf.next()
                                cp('dve', sg.t[:, 0:tn], pb.t[:, 0:tn], [pb.k], [sg.k])
                                S.dma('pool', gkT[f0 - 2560:f0 - 2560 + 128, t0:t0 + tn], sg.t[:, 0:tn], [sg.k], [], sc_)
                            elif kind == 'gz':
                                sg, sc_ = stb.next()
                                cp('dve', sg.t[0:16, 0:tn], pb.t[0:16, 0:tn], [pb.k], [sg.k])
                                S.dma('pool', zT[g, :, t0:t0 + tn], sg.t[0:16, 0:tn], [sg.k], [], sc_)
                            elif kind == 'bg':
                                sg, sc_ = stf.next()
                                act(sg.t[:, 0:tn], pb.t[:, 0:tn], AF.Sigmoid, [pb.k], [sg.k])
                                S.dma('pool', bgT[f0 - 6688:f0 - 6688 + 128, t0:t0 + tn], sg.t[:, 0:tn], [sg.k], [], sc_)
                            else:
                                isq = kind == 'aq'
                                sq, rs, xq, t1 = qtmp[qi % 2]
                                qi += 1
                                act(sq.t[:, 0:tn], pb.t[:, 0:tn], AF.Square, [pb.k], [sq.k])
                                pm2 = PS[0]
                                mm(pm2.t[:, 0:tn], ones_s, sq.t[:, 0:tn], True, True, [cst.k, sq.k], [pm2.k])
                                act(rs.t[:, 0:tn], pm2.t[:, 0:tn], AF.Sqrt, [pm2.k, eps_t.k], [rs.k], bias=eps_t.t[:])
                                op('dve', lambda e, o_=rs.t[:, 0:tn]: e.reciprocal(out=o_, in_=o_), [rs.k], [rs.k])
                                sg, sc_ = stb.next()
                                gcol = qkg.t[:, 0:1] if isq else qkg.t[:, 1:2]
                                if ti == 0:
                                    stt(sg.t[:, 0:tn], pb.t[:, 0:tn], gcol, rs.t[:, 0:tn], ALU.mult, ALU.mult,
                                        [pb.k, qkg.k, rs.k], [sg.k])
                                else:
                                    stt(xq.t[:, 0:tn], pb.t[:, 0:tn], gcol, rs.t[:, 0:tn], ALU.mult, ALU.mult,
                                        [pb.k, qkg.k, rs.k], [xq.k])
                                    pm3 = PS[1]
                                    mm(pm3.t[:, 0:tn], rotT, xq.t[:, 0:tn], True, True, [cst.k, xq.k], [pm3.k])
                                    tt('pool', t1.t[:, 0:tn], xq.t[:, 0:tn], cs_.t[:, 0, 0:tn], ALU.mult, [xq.k, cs_.k], [t1.k])
                                    tt('dve', sq.t[:, 0:tn], pm3.t[:, 0:tn], cs_.t[:, 1, 0:tn], ALU.mult, [pm3.k, cs_.k], [sq.k])
                                    tt('dve', sg.t[:, 0:tn], t1.t[:, 0:tn], sq.t[:, 0:tn], ALU.add, [t1.k, sq.k], [sg.k])
                                if isq:
                                    S.dma('pool', qT[f0 - 5152:f0 - 5152 + 128, t0:t0 + tn], sg.t[:, 0:tn], [sg.k], [], sc_)
                                else:
                                    S.dma('pool', kT[f0 - 6176:f0 - 6176 + 128, t0:t0 + tn], sg.t[:, 0:tn], [sg.k], [], sc_)
                if l + 1 < DEPTH:
                    cast_layer(l + 1)
                S.barrier()

            with contextlib.ExitStack() as ph:
                OC, OL = 1, LC + 4
                lxp = S.sb([128, LC + 4 + N + 2], F32, ph)
                xc = S.sb([128, S_], F32, ph)
                xcb = S.sb([128, S_], BF16, ph)
                hf = S.sb([128, S_], F32, ph)
                tmp = [[S.sb([128, 512], F32, ph) for _ in range(7)] for _ in range(2)]
                gys = Rot(2, [128, 512], F32, ph)
                ysb = Rot(2, [128, 512], BF16, ph, 'sw')
                lch = S.newchan()
                op('pool', lambda e, t_=lxp.t[:]: e.memset(t_, 0.0), [], [lxp.k])
                cwv = pv.t[:, l, PV_CW:PV_CW + 32].rearrange("p (c k) -> p c k", k=4)
                it = 0
                for cg in range(8):
                    S.dma('sp', lxp.t[:, OC:OC + LC], lxT[cg * 128:(cg + 1) * 128, 0:LC], [], [lxp.k], lch)
                    for n0 in range(0, N, 2048):
                        n1 = min(N, n0 + 2048)
                        S.dma('sp', lxp.t[:, OL + n0:OL + n1], lxT[cg * 128:(cg + 1) * 128, LC + n0:LC + n1], [], [lxp.k], lch)
                    for (so, do, ln) in ((OC, 0, LC), (OL, LC, N)):
                        for n0 in range(0, ln, 2048):
                            n1 = min(ln, n0 + 2048)
                            o_ = xc.t[:, do + n0:do + n1]
                            ts('dve', o_, lxp.t[:, so + n0 - 1:so + n1 - 1], cwv[:, cg, 0:1], pv.t[:, l, PV_CB + cg:PV_CB + cg + 1],
                               ALU.mult, ALU.add, [lxp.k, pv.k], [xc.k])
                            for k in range(1, 4):
                                stt(o_, lxp.t[:, so + n0 - 1 + k:so + n1 - 1 + k], cwv[:, cg, k:k + 1], o_, ALU.mult, ALU.add,
                                    [lxp.k, pv.k, xc.k], [xc.k])
                            cp('pool', xcb.t[:, do + n0:do + n1], o_, [xc.k], [xcb.k])
                    prev_hb = None
                    for d in range(2):
                        order = list(range(len(tiles))) if d == 0 else [0] + list(range(len(tiles) - 1, 0, -1))
                        for oi, ti in enumerate(order):
                            t0, tn = tiles[ti]
                            r_, a_, a2_, s_, i_, u_, hb_ = tmp[it % 2]
                            it += 1
                            pa, px = PS[(it % 3) * 2], PS[(it % 3) * 2 + 1]
                            mm(pa.t[:, 0:tn], bd.t[:, d, 0, cg, :], xcb.t[:, t0:t0 + tn], True, True, [bd.k, xcb.k], [pa.k])
                            mm(px.t[:, 0:tn], bd.t[:, d, 1, cg, :], xcb.t[:, t0:t0 + tn], True, True, [bd.k, xcb.k], [px.k])
                            act(r_.t[:, 0:tn], pa.t[:, 0:tn], AF.Sigmoid, [pa.k, pv.k], [r_.k],
                                bias=pv.t[:, l, PV_AB + d * 8 + cg:PV_AB + d * 8 + cg + 1])
                            act(i_.t[:, 0:tn], px.t[:, 0:tn], AF.Sigmoid, [px.k, pv.k], [i_.k],
                                bias=pv.t[:, l, PV_XB + d * 8 + cg:PV_XB + d * 8 + cg + 1])
                            act(a_.t[:, 0:tn], r_.t[:, 0:tn], AF.Exp, [r_.k, lrc.k], [a_.k], scale=lrc.t[:, d, 0, cg:cg + 1])
                            act(a2_.t[:, 0:tn], r_.t[:, 0:tn], AF.Exp, [r_.k, lrc.k], [a2_.k], scale=lrc.t[:, d, 1, cg:cg + 1])
                            act(s_.t[:, 0:tn], a2_.t[:, 0:tn], AF.Sqrt, [a2_.k, one_t.k], [s_.k], bias=one_t.t[:], scale=-1.0)
                            tt('pool', u_.t[:, 0:tn], i_.t[:, 0:tn], xc.t[:, t0:t0 + tn], ALU.mult, [i_.k, xc.k], [u_.k])
                            tt('pool', u_.t[:, 0:tn], u_.t[:, 0:tn], s_.t[:, 0:tn], ALU.mult, [u_.k, s_.k], [u_.k])
                            if d == 0:
                                init = 0.0 if ti == 0 else hf.t[:, t0 - 1:t0]
                                scan(hf.t[:, t0:t0 + tn], a_.t[:, 0:tn], u_.t[:, 0:tn], init, [a_.k, u_.k, hf.k], [hf.k])
                            else:
                                init = 0.0 if oi == 0 else prev_hb.t[:, 0:1]
                                rk = [a_.k, u_.k] + ([prev_hb.k] if oi > 0 else [])
                                scan(hb_.t[:, 0:tn][:, ::-1], a_.t[:, 0:tn][:, ::-1], u_.t[:, 0:tn][:, ::-1], init, rk, [hb_.k])
                                prev_hb = hb_
                                if not (last and ti == 0):
                                    gy, gch = gys.next()
                                    S.dma('sp', gy.t[:, 0:tn], gyT[cg * 128:(cg + 1) * 128, t0:t0 + tn], [], [gy.k], gch)
                                    tt('dve', r_.t[:, 0:tn], hb_.t[:, 0:tn], hf.t[:, t0:t0 + tn], ALU.add, [hb_.k, hf.k], [r_.k])
                                    yb, ych = ysb.next()
                                    tt('dve', yb.t[:, 0:tn], r_.t[:, 0:tn], gy.t[:, 0:tn], ALU.mult, [r_.k, gy.k], [yb.k])
                                    S.dma('pool', y1T[cg * 128:(cg + 1) * 128, t0:t0 + tn], yb.t[:, 0:tn], [yb.k], [], ych)
                S.barrier()

            with contextlib.ExitStack() as ph:
                of = S.sb([128, NCH, 256], F32, ph)
                qs = Rot(2, [128, 512], F32, ph)
                ks = Rot(2, [128, 512], F32, ph)
                zs = Rot(2, [16, 512], BF16, ph)
                vs = Rot(2, [128, 4, 256], BF16, ph)
                srs = Rot(2, [128, 4, 256], F32, ph)
                gtmp = [[S.sb([128, 512], F32, ph) for _ in range(8)] for _ in range(2)]
                btmp = [[S.sb([128, 512], BF16, ph) for _ in range(5)] for _ in range(2)]
                St = S.sb([128, 256], F32, ph)
                Sb = S.sb([128, 256], BF16, ph)
                PTs = [S.sb([128, 128], BF16, ph) for _ in range(2)]
                osum = [S.sb([128, 256], F32, ph) for _ in range(2)]
                gsb = [S.sb([128, 256], F32, ph) for _ in range(2)]
                ybuf = [S.sb([128, 256], F32, ph) for _ in range(2)]
                ss2 = S.sb([128, 2], F32, ph)
                y2s = Rot(2, [128, 2, 512], BF16, ph, 'sw')
                it = 0
                ci = 0
                for h in range(4):
                    for d in range(2):
                        order = list(range(len(tiles))) if d == 0 else [0] + list(range(len(tiles) - 1, 0, -1))
                        op('pool', lambda e, t_=St.t[:]: e.memset(t_, 0.0), [], [St.k])
                        op('pool', lambda e, t_=Sb.t[:]: e.memset(t_, 0.0), [], [Sb.k])
                        il, im = (127, 63) if d == 0 else (0, 64)
                        for ti in order:
                            t0, tn = tiles[ti]
                            nsub = tn // 128
                            G_, B_, D1, D2, E1, E2, E3, E4 = gtmp[it % 2]
                            kh, qt_, kt_, qb_, khT = btmp[it % 2]
                            it += 1
                            q_, qch = qs.next()
                            k_, kch = ks.next()
                            z_, zch = zs.next()
                            v_, vch = vs.next()
                            S.dma('sp', q_.t[:, 0:tn], gqT[h * 128:(h + 1) * 128, t0:t0 + tn], [], [q_.k], qch)
                            S.dma('sp', k_.t[:, 0:tn], gkT[h * 128:(h + 1) * 128, t0:t0 + tn], [], [k_.k], kch)
                            S.dma('sp', z_.t[:, 0:tn], zT[d, :, t0:t0 + tn], [], [z_.k], zch)
                            S.dma('sp', v_.t[:, 0:nsub, :], gvD[t0:t0 + tn, h * 256:(h + 1) * 256].rearrange("(s p) e -> p s e", p=128),
                                  [], [v_.k], vch)
                            if d == 1 and not (last and ti == 0):
                                sr_, sch = srs.next()
                                S.dma('sp', sr_.t[:, 0:nsub, :], srD[t0:t0 + tn, h * 256:(h + 1) * 256].rearrange("(s p) e -> p s e", p=128),
                                      [], [sr_.k], sch)
                            pg = PS[0]
                            mm(pg.t[:, 0:tn], gw.t[0:16, d, h * 128:(h + 1) * 128], z_.t[0:16, 0:tn], True, True, [gw.k, z_.k], [pg.k])
                            act(G_.t[:, 0:tn], pg.t[:, 0:tn], AF.Exp, [pg.k, negb.k], [G_.k], bias=negb.t[:, d * 4 + h:d * 4 + h + 1], scale=-1.0)
                            act(G_.t[:, 0:tn], G_.t[:, 0:tn], AF.Ln, [G_.k, one_t.k], [G_.k], bias=one_t.t[:])
                            if d == 0:
                                scan(B_.t[:, 0:tn], rm.t[:, 0, 0:tn], G_.t[:, 0:tn], 0.0, [rm.k, G_.k], [B_.k])
                            else:
                                scan(B_.t[:, 0:tn][:, ::-1], rm.t[:, 1, 0:tn][:, ::-1], G_.t[:, 0:tn][:, ::-1], 0.0, [rm.k, G_.k], [B_.k])
                            v3 = lambda b_: b_.t[:, 0:tn].rearrange("p (c k) -> p c k", k=128)
                            bl = v3(B_)[:, :, il:il + 1].broadcast_to([128, nsub, 128])
                            bm = v3(B_)[:, :, im:im + 1].broadcast_to([128, nsub, 128])
                            tt('dve', v3(D1), bl, v3(B_), ALU.subtract, [B_.k], [D1.k])
                            tt('dve', v3(D2), bm, v3(B_), ALU.subtract, [B_.k], [D2.k])
                            act(E1.t[:, 0:tn], D1.t[:, 0:tn], AF.Exp, [D1.k], [E1.k], scale=-1.0 / 16)
                            act(E2.t[:, 0:tn], D2.t[:, 0:tn], AF.Exp, [D2.k], [E2.k], scale=1.0 / 16)
                            act(E3.t[:, 0:tn], D2.t[:, 0:tn], AF.Exp, [D2.k], [E3.k], scale=-1.0 / 16)
                            act(E4.t[:, 0:tn], B_.t[:, 0:tn], AF.Exp, [B_.k], [E4.k], scale=-1.0 / 16)
                            tt('pool', kh.t[:, 0:tn], k_.t[:, 0:tn], E1.t[:, 0:tn], ALU.mult, [k_.k, E1.k], [kh.k])
                            tt('dve', qt_.t[:, 0:tn], q_.t[:, 0:tn], E2.t[:, 0:tn], ALU.mult, [q_.k, E2.k], [qt_.k])
                            tt('pool', kt_.t[:, 0:tn], k_.t[:, 0:tn], E3.t[:, 0:tn], ALU.mult, [k_.k, E3.k], [kt_.k])
                            tt('dve', qb_.t[:, 0:tn], q_.t[:, 0:tn], E4.t[:, 0:tn], ALU.mult, [q_.k, E4.k], [qb_.k])
                            for s in range(nsub):
                                tr(PB.t[:, s * 128:(s + 1) * 128], kh.t[:, s * 128:(s + 1) * 128], identb, [kh.k, cstb.k], [PB.k])
                            cp('act', khT.t[:, 0:tn], PB.t[:, 0:tn], [PB.k], [khT.k])
                            wr_y2 = d == 1 and not (last and ti == 0)
                            if wr_y2:
                                y2, y2ch = y2s.next()
                            corder = range(nsub) if d == 0 else range(nsub - 1, -1, -1)
                            for c in corder:
                                cs128 = slice(c * 128, (c + 1) * 128)
                                gc = t0 // 128 + c
                                psc, po, pu = PS[1 + (ci % 2) * 3], PS[2 + (ci % 2) * 3], PS[3 + (ci % 2) * 3]
                                PT = PTs[ci % 2]
                                ci += 1
                                mm(psc.t[:, 0:128], kt_.t[:, cs128], qt_.t[:, cs128], True, True, [kt_.k, qt_.k], [psc.k])
                                tt('dve', PT.t[:], psc.t[:, 0:128], cst.t[:, 3 + d, :], ALU.mult, [psc.k, cst.k], [PT.k])
                                mm(po.t[:, 0:256], qb_.t[:, cs128], Sb.t[:], True, False, [qb_.k, Sb.k], [po.k])
                                mm(po.t[:, 0:256], PT.t[:], v_.t[:, c, :], False, True, [PT.k, v_.k], [po.k])
                                mm(pu.t[:, 0:256], khT.t[:, cs128], v_.t[:, c, :], True, True, [khT.k, v_.k], [pu.k])
                                dcol = E4.t[:, c * 128 + il:c * 128 + il + 1]
                                stt(St.t[:], St.t[:], dcol, pu.t[:, 0:256], ALU.mult, ALU.add, [St.k, E4.k, pu.k], [St.k])
                                cp('act', Sb.t[:], St.t[:], [St.k], [Sb.k])
                                if d == 0:
                                    cp('act', of.t[:, gc, :], po.t[:, 0:256], [po.k], [of.k])
                                elif wr_y2:
                                    os_, gs_, yb_ = osum[ci % 2], gsb[ci % 2], ybuf[ci % 2]
                                    tt('dve', os_.t[:], po.t[:, 0:256], of.t[:, gc, :], ALU.add, [po.k, of.k], [os_.k])
                                    act(yb_.t[:], os_.t[:], AF.Square, [os_.k], [yb_.k, ss2.k], accum=ss2.t[:, 0:1])
                                    act(ss2.t[:, 1:2], ss2.t[:, 0:1], AF.Sqrt, [ss2.k, eps_t.k], [ss2.k], bias=eps_t.t[:], scale=1.0 / 256)
                                    op('dve', lambda e, t_=ss2.t[:, 1:2]: e.reciprocal(out=t_, in_=t_), [ss2.k], [ss2.k])
                                    tt('pool', gs_.t[:], sr_.t[:, c, :], pbc_t.t[:, 2048:2304], ALU.mult, [sr_.k, pbc_t.k], [gs_.k])
                                    stt(yb_.t[:], os_.t[:], ss2.t[:, 1:2], gs_.t[:], ALU.mult, ALU.mult, [os_.k, ss2.k, gs_.k], [yb_.k])
                                    pt2 = PS[0]
                                    for hh in range(2):
                                        tr(pt2.t[:, hh * 128:(hh + 1) * 128], yb_.t[:, hh * 128:(hh + 1) * 128], ident, [yb_.k, cst.k], [pt2.k])
                                    cp('act', y2.t[:, :, cs128], pt2.t[:, 0:256].rearrange("p (a b) -> p a b", a=2), [pt2.k], [y2.k])
                            if wr_y2:
                                S.dma('pool', y2T[h * 256:(h + 1) * 256, t0:t0 + tn].rearrange("(a p) t -> p a t", p=128),
                                      y2.t[:, :, 0:tn], [y2.k], [], y2ch)
                S.barrier()

            with contextlib.ExitStack() as ph:
                kTs = S.sb([128, S_], BF16, ph)
                Vs = S.sb([128, NCH, 128], BF16, ph)
                qts = Rot(2, [128, 512], BF16, ph)
                pTs = [S.sb([128, 512], BF16, ph) for _ in range(4)]
                rinv = S.sb([128, 512], F32, ph)
                y3s = Rot(2, [128, 512], BF16, ph, 'sw')
                accs = [[S.sb([128, 512], F32, ph) for _ in range(2)] for _ in range(2)]
                hil = [[S.sb([128, 512], BF16, ph) for _ in range(2)] for _ in range(2)]
                ech = S.newchan()
                ech2 = S.newchan()
                pi = 0
                qi_ = 0
                for g in range(2):
                    S.dma('sp', kTs.t[:], kT[g * 128:(g + 1) * 128, :], [], [kTs.k], ech)
                    for n0 in range(0, NCH, 8):
                        n1 = min(NCH, n0 + 8)
                        S.dma('sp', Vs.t[:, n0:n1, :], avD[n0 * 128:n1 * 128, g * 128:(g + 1) * 128].rearrange("(n p) e -> p n e", p=128),
                              [], [Vs.k], ech2)
                    for qh in range(4):
                        hh = g * 4 + qh
                        for ti, (t0, tn) in enumerate(tiles):
                            if last and ti == 0:
                                continue
                            nkb = LC // 128 if ti == 0 else NCH
                            qt_, qch = qts.next()
                            S.dma('sp', qt_.t[:, 0:tn], qT[hh * 128:(hh + 1) * 128, t0:t0 + tn], [], [qt_.k], qch)
                            po, psm = PS[qi_ % 2], PS[5 + qi_ % 2]
                            acc = accs[qi_ % 2]
                            hi, lo = hil[qi_ % 2]
                            qi_ += 1
                            pss = [PS[2], PS[3], PS[4]]

                            def score(kb):
                                p_ = pss[kb % 3]
                                mm(p_.t[:, 0:tn], kTs.t[:, kb * 128:(kb + 1) * 128], qt_.t[:, 0:tn], True, True, [kTs.k, qt_.k], [p_.k])
                            score(0)
                            score(1)
                            for kb in range(nkb):
                                if kb + 2 < nkb:
                                    score(kb + 2)
                                p_ = pss[kb % 3]
                                pT = pTs[pi % 4]
                                pi += 1
                                act(pT.t[:, 0:tn], p_.t[:, 0:tn], AF.Exp, [p_.k], [pT.k])
                                mm(po.t[:, 0:tn], Vs.t[:, kb, :], pT.t[:, 0:tn], kb == 0, kb == nkb - 1, [Vs.k, pT.k], [po.k])
                                a_ = acc[kb % 2]
                                eng = 'dve' if kb % 2 == 0 else 'pool'
                                if kb < 2:
                                    cp(eng, a_.t[:, 0:tn], pT.t[:, 0:tn], [pT.k], [a_.k])
                                else:
                                    tt(eng, a_.t[:, 0:tn], a_.t[:, 0:tn], pT.t[:, 0:tn], ALU.add, [a_.k, pT.k], [a_.k])
                            tt('dve', acc[0].t[:, 0:tn], acc[0].t[:, 0:tn], acc[1].t[:, 0:tn], ALU.add, [acc[0].k, acc[1].k], [acc[0].k])
                            cp('dve', hi.t[:, 0:tn], acc[0].t[:, 0:tn], [acc[0].k], [hi.k])
                            tt('dve', lo.t[:, 0:tn], acc[0].t[:, 0:tn], hi.t[:, 0:tn], ALU.subtract, [acc[0].k, hi.k], [lo.k])
                            mm(psm.t[:, 0:tn], onesb, hi.t[:, 0:tn], True, False, [cstb.k, hi.k], [psm.k])
                            mm(psm.t[:, 0:tn], onesb, lo.t[:, 0:tn], False, True, [cstb.k, lo.k], [psm.k])
                            op('dve', lambda e, o_=rinv.t[:, 0:tn], i_=psm.t[:, 0:tn]: e.reciprocal(out=o_, in_=i_), [psm.k], [rinv.k])
                            y3, ych = y3s.next()
                            tt('dve', y3.t[:, 0:tn], po.t[:, 0:tn], rinv.t[:, 0:tn], ALU.mult, [po.k, rinv.k], [y3.k])
                            S.dma('pool', y3T[hh * 128:(hh + 1) * 128, t0:t0 + tn], y3.t[:, 0:tn], [y3.k], [], ych)
                S.barrier()

            with contextlib.ExitStack() as ph:
                wb3 = S.sb([128, 3, 8, D], BF16, ph)
                wo = S.sb([128, 8, D], BF16, ph)
                fch = S.newchan()
                for i in range(3):
                    S.dma('sp', wb3.t[:, i, :, :], w_br16[l, i].rearrange("(c p) n -> p c n", p=128), [('W16', l)], [wb3.k], fch)
                S.dma('sp', wo.t[:], w_out16[l].rearrange("(c p) n -> p c n", p=128), [('W16', l)], [wo.k], S.newchan())
                xts = Rot(2, [128, 4, D], F32, ph, 'both')
                yTs = [Rot(1, [128, 8, 512], BF16, ph) for _ in range(3)]
                g3s = Rot(2, [128, 3, 512], F32, ph)
                mT = S.sb([128, 8, 512], BF16, ph)
                mtmp = [[S.sb([128, 512], F32, ph) for _ in range(2)] for _ in range(2)]
                otmp = [S.sb([128, 512], F32, ph) for _ in range(2)]
                ysrc = (y1T, y2T, y3T)
                pbi = 0
                oi = 0
                for ti, (t0, tn) in enumerate(tiles):
                    if last and ti == 0:
                        continue
                    nsub = tn // 128
                    j = 1 if ti == 0 else 0
                    xt, xch = xts.next()
                    S.dma('sp', xt.t[:, 0:nsub, :], xsrc[t0:t0 + tn, :].rearrange("(s p) d -> p s d", p=128), [xt.k], [xt.k], xch)
                    ys = []
                    for i in range(3):
                        y_, ych = yTs[i].next()
                        S.dma('sp', y_.t[:, :, 0:tn], ysrc[i][:, t0:t0 + tn].rearrange("(c p) t -> p c t", p=128), [], [y_.k], ych)
                        ys.append(y_)
                    for dc in range(8):
                        g3, gch = g3s.next()
                        S.dma('sp', g3.t[:, :, 0:tn], bgT[:, t0:t0 + tn].rearrange("(i c p) t -> c p i t", i=3, p=128)[dc],
                              [], [g3.k], gch)
                        m0, m1 = mtmp[dc % 2]
                        for i in range(3):
                            pb = PS[pbi % 4]
                            pbi += 1
                            for kc in range(8):
                                mm(pb.t[:, 0:tn], wb3.t[:, i, kc, dc * 128:(dc + 1) * 128], ys[i].t[:, kc, 0:tn], kc == 0, kc == 7,
                                   [wb3.k, ys[i].k], [pb.k])
                            if i == 0:
                                tt('dve', m0.t[:, 0:tn], pb.t[:, 0:tn], g3.t[:, 0, 0:tn], ALU.mult, [pb.k, g3.k], [m0.k])
                            elif i == 1:
                                tt('dve', m1.t[:, 0:tn], pb.t[:, 0:tn], g3.t[:, 1, 0:tn], ALU.mult, [pb.k, g3.k], [m1.k])
                                tt('pool', m0.t[:, 0:tn], m0.t[:, 0:tn], m1.t[:, 0:tn], ALU.add, [m0.k, m1.k], [m0.k])
                            else:
                                tt('dve', m1.t[:, 0:tn], pb.t[:, 0:tn], g3.t[:, 2, 0:tn], ALU.mult, [pb.k, g3.k], [m1.k])
                                tt('pool', mT.t[:, dc, 0:tn], m0.t[:, 0:tn], m1.t[:, 0:tn], ALU.add, [m0.k, m1.k], [mT.k])
                    for s in range(nsub):
                        for cb in range(2):
                            pb = PS[4 + pbi % 3]
                            pbi += 1
                            for kc in range(8):
                                mm(pb.t[:], mT.t[:, kc, s * 128:(s + 1) * 128], wo.t[:, kc, cb * 512:(cb + 1) * 512], kc == 0, kc == 7,
                                   [mT.k, wo.k], [pb.k])
                            ot = otmp[oi % 2]
                            oi += 1
                            tt('dve', ot.t[:], pb.t[:], gaTM.t[:, j, 0, cb * 512:(cb + 1) * 512], ALU.mult, [pb.k, gaTM.k], [ot.k])
                            tt('pool', xt.t[:, s, cb * 512:(cb + 1) * 512], ot.t[:], xt.t[:, s, cb * 512:(cb + 1) * 512], ALU.add,
                               [ot.k, xt.k], [xt.k])
                    S.dma('pool', xres[t0:t0 + tn, :].rearrange("(s p) d -> p s d", p=128), xt.t[:, 0:nsub, :], [xt.k], [], xts.store_chan())
                S.barrier()

            with contextlib.ExitStack() as ph:
                xts = Rot(2, [128, 4, D], F32, ph, 'both')
                xn = S.sb([128, 4, D], F32, ph)
                hT = S.sb([128, 8, 512], BF16, ph)
                ssq = S.sb([128, 4], F32, ph)
                rstd = S.sb([128, 4], F32, ph)
                junk = S.sb([128, D], F32, ph)
                wbl = Rot(6, [128, 8, 512], BF16, ph)
                hid = S.sb([128, 32, 512], BF16, ph)
                rtmp = [S.sb([128, 512], F32, ph) for _ in range(2)]
                otmp = [S.sb([128, 512], F32, ph) for _ in range(2)]
                pbi = 0
                ri = 0
                for ti, (t0, tn) in enumerate(tiles):
                    if last and ti == 0:
                        continue
                    nsub = tn // 128
                    j = 1 if ti == 0 else 0
                    xt, xch = xts.next()
                    S.dma('sp', xt.t[:, 0:nsub, :], xres[t0:t0 + tn, :].rearrange("(s p) d -> p s d", p=128), [xt.k], [xt.k], xch)
                    norm_front(ph, xt, nsub, j, 1, hT, xn, ssq, rstd, junk, [PS[0], PS[1]])
                    for blk in range(8):
                        wb_, wch = wbl.next()
                        S.dma('sp', wb_.t[:], w116[l, :, blk * 512:(blk + 1) * 512].rearrange("(c p) n -> p c n", p=128),
                              [('W16', l)], [wb_.k], wch)
                        for g in range(4):
                            pb = PS[2 + pbi % 3]
                            pbi += 1
                            for kc in range(8):
                                mm(pb.t[:, 0:tn], wb_.t[:, kc, g * 128:(g + 1) * 128], hT.t[:, kc, 0:tn], kc == 0, kc == 7,
                                   [wb_.k, hT.k], [pb.k])
                            rt = rtmp[ri % 2]
                            ri += 1
                            act(rt.t[:, 0:tn], pb.t[:, 0:tn], AF.Relu, [pb.k], [rt.k])
                            tt('pool', hid.t[:, blk * 4 + g, 0:tn], rt.t[:, 0:tn], rt.t[:, 0:tn], ALU.mult, [rt.k], [hid.k])
                    for cb in range(2):
                        wbs = []
                        for rg in range(4):
                            wb_, wch = wbl.next()
                            S.dma('sp', wb_.t[:], w216[l, rg * 1024:(rg + 1) * 1024, cb * 512:(cb + 1) * 512].rearrange("(c p) n -> p c n", p=128),
                                  [('W16', l)], [wb_.k], wch)
                            wbs.append(wb_)
                        for s in range(nsub):
                            pb = PS[5 + pbi % 2]
                            pbi += 1
                            for kc in range(32):
                                mm(pb.t[:], hid.t[:, kc, s * 128:(s + 1) * 128], wbs[kc // 8].t[:, kc % 8, :], kc == 0, kc == 31,
                                   [hid.k, wbs[kc // 8].k], [pb.k])
                            ot = otmp[ri % 2]
                            ri += 1
                            tt('dve', ot.t[:], pb.t[:], gaTM.t[:, j, 1, cb * 512:(cb + 1) * 512], ALU.mult, [pb.k, gaTM.k], [ot.k])
                            tt('pool', xt.t[:, s, cb * 512:(cb + 1) * 512], ot.t[:], xt.t[:, s, cb * 512:(cb + 1) * 512], ALU.add,
                               [ot.k, xt.k], [xt.k])
                    if last:
                        dst = out[t0 - LC:t0 - LC + tn, :]
                    else:
                        dst = xres[t0:t0 + tn, :]
                    S.dma('pool', dst.rearrange("(s p) d -> p s d", p=128), xt.t[:, 0:nsub, :], [xt.k], [], xts.store_chan())
                S.barrier()
        S.emit()
    return nc


def _fm(v):
    return np.ascontiguousarray(np.asarray(v, np.float32).reshape(-1, 128).T)


def host_consts(N):
    k = np.arange(128)
    ident = np.eye(128, dtype=np.float32)
    ones_s = np.full((128, 128), 1.0 / 128, np.float32)
    Pm = np.zeros((128, 128), np.float32)
    for base in (0, 64):
        for i in range(32):
            Pm[base + i, base + 32 + i] = -1.0
            Pm[base + 32 + i, base + i] = 1.0
    rotT = np.ascontiguousarray(Pm.T)
    m0 = (k[:, None] <= k[None, :]).astype(np.float32)
    m1 = (k[:, None] >= k[None, :]).astype(np.float32)
    ones = np.ones((128, 128), np.float32)
    consts = np.ascontiguousarray(np.stack([ident, ones_s, rotT, m0, m1, ones], axis=1))
    t = np.arange(512)
    rmask = np.ones((128, 2, 512), np.float32)
    rmask[:, 0, t % 128 == 0] = 0.0
    rmask[:, 1, t % 128 == 127] = 0.0
    rows = N // 64
    row_pos = np.repeat(np.arange(rows), 64).astype(np.float32)
    col_pos = np.tile(np.arange(64), rows).astype(np.float32)
    inv_freq = (np.float32(10000.0) ** (-np.arange(0, 64, 2, dtype=np.float32) / np.float32(64))).astype(np.float32)
    ang_r = (row_pos[:, None] * inv_freq[None, :]).astype(np.float32)
    ang_c = (col_pos[:, None] * inv_freq[None, :]).astype(np.float32)
    cosT = np.concatenate([np.cos(ang_r).T, np.cos(ang_r).T, np.cos(ang_c).T, np.cos(ang_c).T], axis=0)
    sinT = np.concatenate([np.sin(ang_r).T, np.sin(ang_r).T, np.sin(ang_c).T, np.sin(ang_c).T], axis=0)
    return consts, rmask, np.ascontiguousarray(cosT.astype(np.float32)), np.ascontiguousarray(sinT.astype(np.float32))


def host_layout(inp, DEPTH):
    pvec = np.zeros((DEPTH, 128, PV_W), np.float32)
    pbc = np.zeros((DEPTH, 128, PBC_W), np.float32)
    for l in range(DEPTH):
        pvec[l, :, PV_N1G:PV_N1G + 8] = _fm(inp['norm1_g'][l])
        pvec[l, :, PV_N2G:PV_N2G + 8] = _fm(inp['norm2_g'][l])
        mb = np.asarray(inp['mod_b'][l], np.float32)
        for vi, c0 in enumerate((0, 1024, 3072, 4096)):
            pvec[l, :, PV_MODB + vi * 8:PV_MODB + vi * 8 + 8] = _fm(mb[c0:c0 + 1024])
        cw = np.asarray(inp['conv_w'][l], np.float32)
        pvec[l, :, PV_CW:PV_CW + 32] = cw.reshape(4, 8, 128).transpose(2, 1, 0).reshape(128, 32)
        pvec[l, :, PV_CB:PV_CB + 8] = _fm(inp['conv_b'][l])
        for d in range(2):
            pvec[l, :, PV_AB + d * 8:PV_AB + d * 8 + 8] = _fm(inp['lru_a_b'][l, d])
            pvec[l, :, PV_XB + d * 8:PV_XB + d * 8 + 8] = _fm(inp['lru_x_b'][l, d])
            pvec[l, :, PV_LAM + d * 8:PV_LAM + d * 8 + 8] = _fm(inp['lru_lambda'][l, d])
            pvec[l, :, PV_GB + d * 4:PV_GB + d * 4 + 4] = _fm(inp['gla_gate_b'][l, d])
        pvec[l, :, PV_QG] = np.asarray(inp['q_norm_g'][l], np.float32)
        pvec[l, :, PV_KG] = np.asarray(inp['k_norm_g'][l], np.float32)
        pbc[l, :, 0:1024] = mb[None, 2048:3072]
        pbc[l, :, 1024:2048] = mb[None, 5120:6144]
        pbc[l, :, 2048:2304] = np.asarray(inp['gla_norm_g'][l], np.float32)[None, :]
    return pvec, pbc


_CACHE = {}


def run(inputs, N, LC, DEPTH, ncores, debug=False):
    key = (N, LC, DEPTH, debug)
    if key not in _CACHE:
        _CACHE[key] = build(N, LC, DEPTH, debug)
    nc = _CACHE[key]
    f = lambda a: np.ascontiguousarray(np.asarray(a, np.float32))
    consts, rmask, cosT, sinT = host_consts(N)
    pvec, pbc = host_layout(inputs, DEPTH)
    shared = dict(pvec=pvec, pbc=pbc, consts=consts, rmask=rmask, cosT=cosT, sinT=sinT)
    for k in ('mod_w', 'w_in', 'lru_a_w', 'lru_x_w', 'gla_gate_w', 'w_branch', 'w_out', 'mlp_w1', 'mlp_w2'):
        shared[k] = f(inputs[k])[:DEPTH]
    in_maps = []
    for b in range(ncores):
        m = dict(shared)
        m['xin'] = np.ascontiguousarray(np.concatenate([f(inputs['ctx'][b]), f(inputs['x'][b])], axis=0))
        ccT = np.stack([_fm(inputs['c'][b]), _fm(inputs['c_ctx'])], axis=2)
        m['ccT'] = np.ascontiguousarray(ccT)
        in_maps.append(m)
    res = run_bass_kernel_spmd(nc, in_maps, core_ids=list(range(ncores)))
    return res


def kernel(**inputs):
    B, N, _ = inputs['x'].shape
    LC = inputs['ctx'].shape[1]
    DEPTH = inputs['w_in'].shape[0]
    res = run(inputs, N, LC, DEPTH, B)
    return np.stack([np.asarray(r['out'], np.float32) for r in res.results], axis=0)
```
